# Optimizing a Trainium2 kernel written in Bass

```python
import math
import jax, jax.numpy as jnp
from jax import lax
import numpy as np

D_MODEL = 1024
BATCH = 32
SEQ = 2048
DEPTH = 2
DEC_BATCH = 1
DEC_SEQ = 16384
PAST_LEN = 128

HEAD_DIM = 64
N_Q_HEADS = 12
N_KV_HEADS_A = 4
N_MEM_HEADS = 4
MEM_LEN = 256
WINDOW = 128
BLOCK = 128
GRID_W = 64
NA_ROWS_MAX = 8
NA_COLS = 16
D_FF = 2816
CONV_W = 3
EPS = 1e-6
NEG = -1e30
N_LAYERS_A = (DEPTH + 1) // 2
N_LAYERS_B = DEPTH // 2
Q_WIDTH = N_Q_HEADS * HEAD_DIM
KV_WIDTH_A = N_KV_HEADS_A * HEAD_DIM
MEM_WIDTH = N_MEM_HEADS * HEAD_DIM
IN_WIDTH_A = Q_WIDTH + 2 * KV_WIDTH_A + MEM_WIDTH
IN_WIDTH_B = 3 * Q_WIDTH + MEM_WIDTH
MIX_WIDTH = Q_WIDTH + MEM_WIDTH
RPB_ROWS = 2 * NA_ROWS_MAX - 1
RPB_COLS = 2 * NA_COLS - 1

kernel_name = "hybrid_window_gqa_natten_memory_convffn_encoder"


def rms_norm(x, g):
    xf = x.astype(jnp.float32)
    y = xf * lax.rsqrt(jnp.mean(xf * xf, axis=-1, keepdims=True) + EPS)
    return (y * g.astype(jnp.float32)).astype(x.dtype)


def alibi_slopes(n):
    return jnp.asarray((2.0 ** (-8.0 * np.arange(1, n + 1, dtype=np.float32) / n)).astype(np.float32))


def window_gqa(q, k, v, sink):
    B, T, Hq, d = q.shape
    Hkv = k.shape[2]
    rep = Hq // Hkv
    nb = T // BLOCK
    kb_len = BLOCK + 2 * WINDOW
    kp = jnp.pad(k, ((0, 0), (WINDOW, WINDOW), (0, 0), (0, 0)))
    vp = jnp.pad(v, ((0, 0), (WINDOW, WINDOW), (0, 0), (0, 0)))
    qg = q.reshape(B, T, Hkv, rep, d)
    rel = jnp.arange(BLOCK)[:, None] - jnp.arange(kb_len)[None, :] + WINDOW
    in_window = jnp.abs(rel) <= WINDOW
    alibi = -alibi_slopes(Hq).reshape(Hkv, rep, 1, 1) * jnp.abs(rel).astype(jnp.float32)
    sink_f = sink.astype(jnp.float32).reshape(Hkv, rep, 1)
    scale = HEAD_DIM ** -0.5

    def one_block(j):
        start = j * BLOCK
        qb = lax.dynamic_slice_in_dim(qg, start, BLOCK, axis=1).astype(jnp.float32)
        kb = lax.dynamic_slice_in_dim(kp, start, kb_len, axis=1).astype(jnp.float32)
        vb = lax.dynamic_slice_in_dim(vp, start, kb_len, axis=1).astype(jnp.float32)
        s_pos = start - WINDOW + jnp.arange(kb_len)
        valid = in_window & ((s_pos >= 0) & (s_pos < T))[None, :]
        sc = jnp.einsum('bqgrd,bkgd->bgrqk', qb, kb) * scale + alibi
        sc = jnp.where(valid, sc, NEG)
        m = jnp.maximum(sc.max(axis=-1), sink_f)
        p = jnp.exp(sc - m[..., None])
        denom = p.sum(axis=-1) + jnp.exp(sink_f - m)
        o = jnp.einsum('bgrqk,bkgd->bqgrd', p, vb)
        o = o / jnp.transpose(denom, (0, 3, 1, 2))[..., None]
        return o.astype(q.dtype)

    out = lax.map(one_block, jnp.arange(nb))
    return jnp.moveaxis(out, 0, 1).reshape(B, T, Hq * d)


def neighbourhood_attn(q, k, v, rpb):
    B, T, H, d = q.shape
    rows = T // GRID_W
    kr = min(NA_ROWS_MAX, rows)
    qg = q.reshape(B, rows, GRID_W, H, d)
    kg = k.reshape(B, rows, GRID_W, H, d)
    vg = v.reshape(B, rows, GRID_W, H, d)
    cols = np.arange(GRID_W)
    c0 = np.clip(cols - NA_COLS // 2, 0, GRID_W - NA_COLS)
    col_idx = c0[:, None] + np.arange(NA_COLS)[None, :]
    dc = col_idx - cols[:, None]
    bias_cols = rpb.astype(jnp.float32)[:, :, dc + NA_COLS - 1]
    scale = HEAD_DIM ** -0.5

    def one_row(r):
        r0 = jnp.clip(r - kr // 2, 0, rows - kr)
        qr = lax.dynamic_index_in_dim(qg, r, axis=1, keepdims=False).astype(jnp.float32)
        kw = lax.dynamic_slice_in_dim(kg, r0, kr, axis=1)
        vw = lax.dynamic_slice_in_dim(vg, r0, kr, axis=1)
        kq = kw[:, :, col_idx].astype(jnp.float32)
        vq = vw[:, :, col_idx].astype(jnp.float32)
        dr = r0 + jnp.arange(kr) - r + NA_ROWS_MAX - 1
        bias = jnp.transpose(bias_cols[:, dr], (0, 2, 1, 3))
        sc = jnp.einsum('bchd,bicjhd->bhcij', qr, kq) * scale + bias[None]
        p = jax.nn.softmax(sc.reshape(B, H, GRID_W, kr * NA_COLS), axis=-1)
        p = p.reshape(B, H, GRID_W, kr, NA_COLS)
        o = jnp.einsum('bhcij,bicjhd->bchd', p, vq)
        return o.astype(q.dtype)

    out = lax.map(one_row, jnp.arange(rows))
    return jnp.moveaxis(out, 0, 1).reshape(B, T, H * d)


def memory_attn(qm, mem_n, w_mem_kv):
    B, T, Hm, d = qm.shape
    kv = mem_n @ w_mem_kv
    km, vm = jnp.split(kv, 2, axis=-1)
    km = km.reshape(B, -1, Hm, d).astype(jnp.float32)
    vm = vm.reshape(B, -1, Hm, d).astype(jnp.float32)
    sc = jnp.einsum('bthd,bmhd->bhtm', qm.astype(jnp.float32), km) * (HEAD_DIM ** -0.5)
    p = jax.nn.softmax(sc, axis=-1)
    o = jnp.einsum('bhtm,bmhd->bthd', p, vm)
    return o.reshape(B, T, Hm * d).astype(qm.dtype)


def conv_ffn(h, w_gate, w_up, conv_w, conv_b, w_down):
    T = h.shape[1]
    g = h @ w_gate
    pad = CONV_W // 2
    gp = jnp.pad(g, ((0, 0), (pad, pad), (0, 0)))
    g = sum(gp[:, tap:tap + T] * conv_w[tap] for tap in range(CONV_W)) + conv_b
    return (jax.nn.silu(g) * (h @ w_up)) @ w_down


def trunk(x, mem, g_mix, g_mem, w_in_a, sink_a, w_in_b, rpb_b, w_mem_kv, w_o,
          g_ffn, w_gate, w_up, conv_w, conv_b, w_down, g_final):
    B, T, _ = x.shape
    for i in range(DEPTH):
        h = rms_norm(x, g_mix[i])
        mem_n = rms_norm(mem, g_mem[i])
        j = i // 2
        if i % 2 == 0:
            proj = h @ w_in_a[j]
            q, k, v, qm = jnp.split(proj, [Q_WIDTH, Q_WIDTH + KV_WIDTH_A, Q_WIDTH + 2 * KV_WIDTH_A], axis=-1)
            o_mix = window_gqa(q.reshape(B, T, N_Q_HEADS, HEAD_DIM),
                               k.reshape(B, T, N_KV_HEADS_A, HEAD_DIM),
                               v.reshape(B, T, N_KV_HEADS_A, HEAD_DIM), sink_a[j])
        else:
            proj = h @ w_in_b[j]
            q, k, v, qm = jnp.split(proj, [Q_WIDTH, 2 * Q_WIDTH, 3 * Q_WIDTH], axis=-1)
            o_mix = neighbourhood_attn(q.reshape(B, T, N_Q_HEADS, HEAD_DIM),
                                       k.reshape(B, T, N_Q_HEADS, HEAD_DIM),
                                       v.reshape(B, T, N_Q_HEADS, HEAD_DIM), rpb_b[j])
        o_mem = memory_attn(qm.reshape(B, T, N_MEM_HEADS, HEAD_DIM), mem_n, w_mem_kv[i])
        x = x + jnp.concatenate([o_mix, o_mem], axis=-1) @ w_o[i]
        h = rms_norm(x, g_ffn[i])
        x = x + conv_ffn(h, w_gate[i], w_up[i], conv_w[i], conv_b[i], w_down[i])
    return rms_norm(x, g_final)


def setup_inputs(seed: int = 0) -> dict:
    key = jax.random.key(seed)
    ks = jax.random.split(key, 20)
    f32 = jnp.float32

    def nrm(k, shape, scale):
        return jax.random.normal(k, shape, f32) * scale

    return {
        "x_prompt": nrm(ks[0], (BATCH, SEQ, D_MODEL), 1.0),
        "x_sample": nrm(ks[1], (DEC_BATCH, DEC_SEQ, D_MODEL), 1.0),
        "mem_prompt": nrm(ks[2], (BATCH, MEM_LEN, D_MODEL), 1.0),
        "mem_sample": nrm(ks[3], (DEC_BATCH, MEM_LEN, D_MODEL), 1.0),
        "g_mix": 1.0 + nrm(ks[4], (DEPTH, D_MODEL), 0.02),
        "g_mem": 1.0 + nrm(ks[5], (DEPTH, D_MODEL), 0.02),
        "w_in_a": nrm(ks[6], (N_LAYERS_A, D_MODEL, IN_WIDTH_A), D_MODEL ** -0.5),
        "sink_a": nrm(ks[7], (N_LAYERS_A, N_Q_HEADS), 0.5),
        "w_in_b": nrm(ks[8], (N_LAYERS_B, D_MODEL, IN_WIDTH_B), D_MODEL ** -0.5),
        "rpb_b": nrm(ks[9], (N_LAYERS_B, N_Q_HEADS, RPB_ROWS, RPB_COLS), 0.1),
        "w_mem_kv": nrm(ks[10], (DEPTH, D_MODEL, 2 * MEM_WIDTH), D_MODEL ** -0.5),
        "w_o": nrm(ks[11], (DEPTH, MIX_WIDTH, D_MODEL), MIX_WIDTH ** -0.5),
        "g_ffn": 1.0 + nrm(ks[12], (DEPTH, D_MODEL), 0.02),
        "w_gate": nrm(ks[13], (DEPTH, D_MODEL, D_FF), D_MODEL ** -0.5),
        "w_up": nrm(ks[14], (DEPTH, D_MODEL, D_FF), D_MODEL ** -0.5),
        "conv_w": nrm(ks[15], (DEPTH, CONV_W, D_FF), CONV_W ** -0.5),
        "conv_b": nrm(ks[16], (DEPTH, D_FF), 0.01),
        "w_down": nrm(ks[17], (DEPTH, D_FF, D_MODEL), D_FF ** -0.5),
        "g_final": 1.0 + nrm(ks[18], (D_MODEL,), 0.02),
    }


def reference(x_prompt, x_sample, mem_prompt, mem_sample, g_mix, g_mem, w_in_a, sink_a, w_in_b, rpb_b,
              w_mem_kv, w_o, g_ffn, w_gate, w_up, conv_w, conv_b, w_down, g_final):
    y_prompt = trunk(x_prompt, mem_prompt, g_mix, g_mem, w_in_a, sink_a, w_in_b, rpb_b, w_mem_kv, w_o,
                     g_ffn, w_gate, w_up, conv_w, conv_b, w_down, g_final)
    y_sample = trunk(x_sample, mem_sample, g_mix, g_mem, w_in_a, sink_a, w_in_b, rpb_b, w_mem_kv, w_o,
                     g_ffn, w_gate, w_up, conv_w, conv_b, w_down, g_final)
    return (y_prompt, y_sample)
```

```python
import numpy as np
from contextlib import ExitStack
import concourse.bass as bass
import concourse.mybir as mybir
from concourse.bass_utils import run_bass_kernel_spmd

F32, BF16 = mybir.dt.float32, mybir.dt.bfloat16
AF = mybir.ActivationFunctionType
ALU = mybir.AluOpType
AX = mybir.AxisListType
NEG = -30000.0
NCORES = 8
D = 1024
DFF = 2816
NFC = 22
EPS = 1e-6
NT_P, NT_S = 16, 18
S_OFF = 5
EDGE_SLOTS_S = (5, 6, 11, 12)
EDGE_WIN_S = {5: (3, 9), 6: (4, 9), 11: (9, 14), 12: (9, 15)}
EDGE_SLOTS_P = (0, 1, 14, 15)
EDGE_WIN_P = {0: (0, 4), 1: (0, 4), 14: (12, 16), 15: (12, 16)}


class Res:
    __slots__ = ("name", "w", "rs", "sem", "cnt")

    def __init__(self, name):
        self.name = name
        self.w = None
        self.rs = []
        self.sem = None
        self.cnt = 0


class Tr:
    def __init__(self, nc, stack):
        self.nc = nc
        self.stack = stack
        self.engs = {"pe": nc.tensor, "act": nc.scalar, "dve": nc.vector, "pool": nc.gpsimd, "sp": nc.sync}
        self.sem = {k: stack.enter_context(nc.semaphore("sem_" + k)) for k in self.engs}
        self.cnt = {k: 0 for k in self.engs}
        self.waited = {k: {} for k in self.engs}
        self.dirty = {}
        self.dsem = {}
        self.ninst = 0

    def res(self, name):
        return Res(name)

    def _wait(self, eng, sem, val):
        w = self.waited[eng]
        k = id(sem)
        if w.get(k, 0) >= val:
            return
        w[k] = val
        self.engs[eng].wait_ge(sem, val)
        self.ninst += 1

    def _deps(self, eng, reads, writes, is_dma):
        need = {}

        def add(tok, kind):
            sem, val, teng = tok
            if (not is_dma) and teng == eng:
                if eng == "pe" or kind != "raw":
                    return
            k = id(sem)
            if k not in need or need[k][1] < val:
                need[k] = (sem, val)

        for r in reads:
            if r.w is not None:
                add(r.w, "raw")
        for w in writes:
            if w.w is not None:
                add(w.w, "waw")
            for t in w.rs:
                add(t, "war")
        for sem, val in need.values():
            self._wait(eng, sem, val)

    def _commit(self, tok, reads, writes):
        for r in reads:
            rs = [t for t in r.rs if t[0] is not tok[0]]
            rs.append(tok)
            r.rs = rs
            self.dirty[id(r)] = r
        for w in writes:
            w.w = tok
            w.rs = []
            self.dirty[id(w)] = w

    def op(self, eng, fn, reads=(), writes=()):
        self._deps(eng, reads, writes, False)
        ins = fn(self.engs[eng])
        self.cnt[eng] += 1
        ins.then_inc(self.sem[eng], 1)
        self.ninst += 1
        self._commit((self.sem[eng], self.cnt[eng], eng), reads, writes)

    def dma(self, q, out, in_, owner, reads=(), writes=(), nowaw=False, **kw):
        self._deps(q, reads, () if nowaw else writes, True)
        h = self.dsem.get(owner.name)
        if h is None:
            h = [self.stack.enter_context(self.nc.semaphore("dsem_%s" % owner.name)), 0]
            self.dsem[owner.name] = h
        ins = self.engs[q].dma_start(out=out, in_=in_, **kw)
        h[1] += 16
        ins.then_inc(h[0], 16)
        self.ninst += 1
        self._commit((h[0], h[1], None), reads, writes)

    def barrier(self):
        for e in self.engs:
            for f in self.engs:
                if self.cnt[f] > 0:
                    self._wait(e, self.sem[f], self.cnt[f])
            for nm, h in self.dsem.items():
                if not nm.startswith("wcast"):
                    self._wait(e, h[0], h[1])
        keep = {}
        for k, r in self.dirty.items():
            if r.name.startswith("wcast"):
                r.rs = []
                keep[k] = r
                continue
            r.w = None
            r.rs = []
        self.dirty = keep


def _alibi_table():
    slopes = (2.0 ** (-8.0 * np.arange(1, 13, dtype=np.float32) / 12)).astype(np.float32)
    p = np.arange(128)[:, None, None]
    j = np.arange(3)[None, :, None] - 1
    q = np.arange(128)[None, None, :]
    rel = q - (p + 128 * j)
    a = np.abs(rel).astype(np.float32)
    tab = -slopes[None, :, None, None] * a[:, None, :, :]
    tab = np.where((a <= 128)[:, None, :, :], tab, np.float32(NEG)).astype(np.float32)
    return np.ascontiguousarray(tab)


def _na_index(gq, gks, R):
    p = np.arange(128)
    krl, kc = p // 64, p % 64
    q = np.arange(128)
    qrl, qc = q // 64, q % 64
    out = np.full((128, len(gks), 128), 465, dtype=np.int64)
    r = 2 * gq + qrl
    r0 = np.clip(r - 4, 0, R - 8)
    c0 = np.clip(qc - 8, 0, 48)
    for ji, gk in enumerate(gks):
        kr = 2 * gk + krl
        vr = (kr[:, None] >= r0[None, :]) & (kr[:, None] < r0[None, :] + 8) & (kr[:, None] >= 0) & (kr[:, None] < R)
        vc = (kc[:, None] >= c0[None, :]) & (kc[:, None] < c0[None, :] + 16)
        dr = kr[:, None] - r[None, :] + 7
        dc = kc[:, None] - qc[None, :] + 15
        ok = vr & vc & (dr >= 0) & (dr < 15) & (dc >= 0) & (dc < 31)
        idx = np.where(ok, dr * 31 + dc, 465)
        out[:, ji, :] = idx
    return out


def _gather_na(rpb_pad, idx):
    t = rpb_pad[:, idx]
    return np.ascontiguousarray(np.transpose(t, (1, 0, 2, 3)))


def build_program():
    nc = bass.Bass("TRN2", target_bir_lowering=False)
    st = ExitStack()
    T = Tr(nc, st)

    def din(name, shape, dt=F32):
        return nc.dram_tensor(name, list(shape), dt, kind="ExternalInput").ap()

    xp = din("xp", [4, NT_P * 128, D])
    xs_in = din("xs", [2, NT_S * 128, D])
    memp = din("memp", [4, 256, D])
    mems = din("mems", [256, D])
    vflag = din("vflag", [128, 3 * NT_S])
    gcols_d = din("gcols", [128, 7 * 8])
    gfin_d = din("gfin", [128, D])
    convp_d = din("convp", [128, 2 * NFC * 4])
    sink_d = din("sinkb", [128, 12])
    alibi_d = din("alibi", [128, 12 * 3 * 128])
    alibi2_d = din("alibi2", [128, 12 * 3 * 128])
    naint_d = din("naint", [128, 12 * 5 * 128])
    edgep_d = din("edgep", [4, 128, 12 * 4 * 128])
    edges_d = din("edges", [2, 4, 128, 12 * 6 * 128])
    w_in_a = din("w_in_a", [D, 1536])
    w_in_b = din("w_in_b", [D, 2560])
    w_mkv = din("w_mem_kv", [2, D, 512])
    w_o = din("w_o", [2, D, D])
    w_g = din("w_gate", [2, D, DFF])
    w_u = din("w_up", [2, D, DFF])
    w_d = din("w_down", [2, DFF, D])
    yp = nc.dram_tensor("yp", [4, NT_P * 128, D], F32, kind="ExternalOutput").ap()
    ys = nc.dram_tensor("ys", [2, 8 * 128, D], F32, kind="ExternalOutput").ap()

    def dscr(name, shape, dt=BF16):
        return nc.dram_tensor(name, list(shape), dt).ap()

    ws_in = [dscr("ws_in0", [6 + 4 + 2, 128, 8, 128]), dscr("ws_in1", [6 + 6 + 2, 128, 8, 128])]
    ws_mk = dscr("ws_mk", [2, 2, 128, 8, 128])
    ws_g = dscr("ws_g", [2, NFC, 128, 8, 128])
    ws_u = dscr("ws_u", [2, NFC, 128, 8, 128])
    wm_v = [dscr("wm_v0", [D, 256]), dscr("wm_v1", [D, 768])]
    wm_mv = dscr("wm_mv", [2, D, 256])
    wm_o = dscr("wm_o", [2, D, D])
    wm_d = dscr("wm_d", [2, DFF, D])
    xm_d = dscr("xm_d", [NT_S * 128, D], F32)
    x1_d = dscr("x1_d", [NT_S * 128, D], F32)
    R_xm, R_x1 = T.res("xm_d"), T.res("x1_d")
    R_w = T.res("wscr")

    RW = {(k, l): T.res("wcast_%s%d" % (k, l)) for k in ("in", "o", "gu", "d") for l in range(2)}

    def cast_stat(dst, W, col0, R):
        src = W[:, col0:col0 + 128].rearrange("(c p) n -> p c n", p=128)
        T.dma("pool", dst, src, R, writes=[R], nowaw=True)

    def cast_stat_dup(dst, W, col0, R):
        src = W[:, col0:col0 + 64].rearrange("(c p) n -> p c n", p=128)
        T.dma("pool", dst[:, :, 0:64], src, R, writes=[R], nowaw=True)
        T.dma("pool", dst[:, :, 64:128], src, R, writes=[R], nowaw=True)

    def cast_mov(dst, W, rows, R):
        for r0 in range(0, rows, 512):
            r1 = min(rows, r0 + 512)
            T.dma("pool", dst[r0:r1, :], W[r0:r1, :], R, writes=[R], nowaw=True)

    def prologue_casts():
        for l in range(2):
            R = RW[("in", l)]
            if l == 0:
                wa = w_in_a
                for o in range(6):
                    cast_stat(ws_in[0][o], wa, o * 128, R)
                for g in range(4):
                    cast_stat_dup(ws_in[0][6 + g], wa, 768 + g * 64, R)
                for o in range(2):
                    cast_stat(ws_in[0][10 + o], wa, 1280 + o * 128, R)
                cast_mov(wm_v[0], wa[:, 1024:1280], D, R)
            else:
                wb = w_in_b
                for o in range(12):
                    cast_stat(ws_in[1][o], wb, o * 128, R)
                for o in range(2):
                    cast_stat(ws_in[1][12 + o], wb, 2304 + o * 128, R)
                cast_mov(wm_v[1], wb[:, 1536:2304], D, R)
            for o in range(2):
                cast_stat(ws_mk[l, o], w_mkv[l], o * 128, R)
            cast_mov(wm_mv[l], w_mkv[l][:, 256:512], D, R)
            cast_mov(wm_o[l], w_o[l], D, RW[("o", l)])
            for c in range(NFC):
                cast_stat(ws_g[l, c], w_g[l], c * 128, RW[("gu", l)])
                cast_stat(ws_u[l, c], w_u[l], c * 128, RW[("gu", l)])
            cast_mov(wm_d[l], w_d[l], DFF, RW[("d", l)])

    _uid = [0]

    def sb(name, shape, dt, stack=st):
        _uid[0] += 1
        return stack.enter_context(nc.sbuf_tensor("%s_u%d" % (name, _uid[0]), list(shape), dt))

    ident = sb("ident", [128, 128], BF16)
    identf = sb("identf", [128, 128], F32)
    gcols = sb("gcols_sb", [128, 56], F32)
    convp = sb("convp_sb", [128, 2 * NFC * 4], F32)
    expsink = sb("expsink", [128, 12], F32)
    vfl = sb("vfl", [128, 3 * NT_S], F32)
    R_const = T.res("const")
    T.op("pool", lambda e: e.memset(identf[:], 1.0), writes=[R_const])
    T.op("pool", lambda e: e.affine_select(out=identf[:], in_=identf[:], pattern=[[-1, 128]],
                                           compare_op=ALU.is_equal, fill=0.0, base=0, channel_multiplier=1),
         reads=[R_const], writes=[R_const])
    T.op("pool", lambda e: e.tensor_copy(out=ident[:], in_=identf[:]), reads=[R_const], writes=[R_const])
    epsc = sb("epsc", [128, 1], F32)
    T.op("pool", lambda e: e.memset(epsc[:], EPS), writes=[R_const])
    R_cl = T.res("constload")
    T.dma("sp", gcols[:], gcols_d[:, :], R_cl, writes=[R_cl])
    T.dma("sp", convp[:], convp_d[:, :], R_cl, writes=[R_cl])
    T.dma("sp", expsink[:], sink_d[:, :], R_cl, writes=[R_cl])
    T.dma("sp", vfl[:], vflag[:, :], R_cl, writes=[R_cl])
    T.op("act", lambda e: e.activation(out=expsink[:], in_=expsink[:], func=AF.Exp), reads=[R_cl], writes=[R_cl])

    ps_all = st.enter_context(nc.psum_tensor("ps_all", [128, 4096], F32))
    psq = [ps_all[:, i * 1024:(i + 1) * 1024] for i in range(4)]
    R_ps = [T.res("psbank%d" % i) for i in range(8)]

    def bank(i):
        return psq[i // 2][:, (i % 2) * 512:(i % 2) * 512 + 512], R_ps[i]

    prologue_casts()

    GM = [0, 8]
    GMEM = [16, 24]
    GFFN = [32, 40]

    def rms_rstd(ph, xs_ap, np_, tag, Rx, bufs):
        junk, ss, R = bufs
        T.op("act", lambda e: e.activation(out=junk[0:np_, :], in_=xs_ap, func=AF.Square, accum_out=ss[0:np_, 0:1]),
             reads=[Rx], writes=[R])
        T.op("act", lambda e: e.activation(out=ss[0:np_, 2:3], in_=ss[0:np_, 0:1], func=AF.Ln, scale=1.0 / D, bias=epsc[0:np_, 0:1]),
             reads=[R, R_const], writes=[R])
        T.op("act", lambda e: e.activation(out=ss[0:np_, 3:4], in_=ss[0:np_, 2:3], func=AF.Exp, scale=-0.5), reads=[R], writes=[R])
        return ss[0:np_, 3:4], R

    _nt = [0]

    def norm_transpose(xs_ap, Rx, np_, goff, dst3, Rdst, nb, pbank):
        junk, ss, Rn, hb, Rhb = nb
        rstd, Rn_ = rms_rstd(None, xs_ap, np_, "", Rx, (junk, ss, Rn))
        _nt[0] += 1
        if _nt[0] % 2:
            T.op("act", lambda e: e.activation(out=hb[0:np_, :], in_=xs_ap, func=AF.Copy, scale=rstd),
                 reads=[Rx, Rn], writes=[Rhb])
        else:
            T.op("dve", lambda e: e.tensor_scalar(out=hb[0:np_, :], in0=xs_ap, scalar1=rstd, scalar2=None, op0=ALU.mult),
                 reads=[Rx, Rn], writes=[Rhb])
        pst = psq[pbank]
        Rp = [R_ps[2 * pbank], R_ps[2 * pbank + 1]]
        for c in range(8):
            T.op("pe", lambda e, c=c: e.matmul(pst[:, c * np_:(c + 1) * np_] if np_ < 128 else pst[:, c * 128:(c + 1) * 128],
                                               lhsT=hb[0:np_, c * 128:(c + 1) * 128], rhs=ident[0:np_, 0:np_],
                                               start=True, stop=True),
                 reads=[Rhb, R_const], writes=Rp)
        if np_ == 128:
            src = pst[:, 0:1024].rearrange("p (c n) -> p c n", c=8)
        else:
            src = pst[:, 0:8 * np_].rearrange("p (c n) -> p c n", c=8)
        gb = gcols[:, goff:goff + 8].unsqueeze(2).broadcast_to([128, 8, np_])
        T.op("dve", lambda e: e.tensor_tensor(out=dst3, in0=src, in1=gb, op=ALU.mult), reads=Rp + [R_cl], writes=[Rdst])

    def norm_pipeline(items, xbufs, R_xbufs, nbs_, pbanks):
        n = len(items)
        NBF = len(xbufs)

        def load(s_):
            it = items[s_]
            b_ = s_ % NBF
            it["xs"], it["Rx"], it["nb"], it["pb"] = xbufs[b_], R_xbufs[b_], nbs_[b_], pbanks[s_ % len(pbanks)]
            if it.get("src") is not None:
                T.dma("sp", xbufs[b_][0:it["np"], :], it["src"], R_xbufs[b_], reads=it.get("rsrc", []), writes=[R_xbufs[b_]])
            if it.get("pre") is not None:
                it["pre"](xbufs[b_], R_xbufs[b_])

        def st1(it):
            junk, ss, Rn, hb, Rhb = it["nb"]
            np_ = it["np"]
            xs_ap = it["xs"][0:np_, :]
            T.op("act", lambda e: e.activation(out=junk[0:np_, :], in_=xs_ap, func=AF.Square, accum_out=ss[0:np_, 0:1]),
                 reads=[it["Rx"]], writes=[Rn])

        def st2(it):
            junk, ss, Rn, hb, Rhb = it["nb"]
            np_ = it["np"]
            xs_ap = it["xs"][0:np_, :]
            T.op("act", lambda e: e.activation(out=ss[0:np_, 2:3], in_=ss[0:np_, 0:1], func=AF.Ln, scale=1.0 / D, bias=epsc[0:np_, 0:1]),
                 reads=[Rn, R_const], writes=[Rn])
            T.op("act", lambda e: e.activation(out=ss[0:np_, 3:4], in_=ss[0:np_, 2:3], func=AF.Exp, scale=-0.5), reads=[Rn], writes=[Rn])
            rstd = ss[0:np_, 3:4]
            _nt[0] += 1
            if _nt[0] % 2:
                T.op("act", lambda e: e.activation(out=hb[0:np_, :], in_=xs_ap, func=AF.Copy, scale=rstd),
                     reads=[it["Rx"], Rn], writes=[Rhb])
            else:
                T.op("dve", lambda e: e.tensor_scalar(out=hb[0:np_, :], in0=xs_ap, scalar1=rstd, scalar2=None, op0=ALU.mult),
                     reads=[it["Rx"], Rn], writes=[Rhb])

        def st3(it):
            junk, ss, Rn, hb, Rhb = it["nb"]
            np_ = it["np"]
            pst = psq[it["pb"]]
            Rp = [R_ps[2 * it["pb"]], R_ps[2 * it["pb"] + 1]]
            for c in range(8):
                T.op("pe", lambda e, c=c: e.matmul(pst[:, c * np_:(c + 1) * np_], lhsT=hb[0:np_, c * 128:(c + 1) * 128],
                                                   rhs=ident[0:np_, 0:np_], start=True, stop=True),
                     reads=[Rhb, R_const], writes=Rp)

        def st4(it):
            np_ = it["np"]
            pst = psq[it["pb"]]
            Rp = [R_ps[2 * it["pb"]], R_ps[2 * it["pb"] + 1]]
            src = pst[:, 0:8 * np_].rearrange("p (c n) -> p c n", c=8)
            gb = gcols[:, it["goff"]:it["goff"] + 8].unsqueeze(2).broadcast_to([128, 8, np_])
            T.op("dve", lambda e: e.tensor_tensor(out=it["dst"], in0=src, in1=gb, op=ALU.mult), reads=Rp + [R_cl], writes=[it["Rdst"]])

        for s_ in range(min(2, n)):
            load(s_)
        for s_ in range(n + 3):
            if s_ + 2 < n:
                load(s_ + 2)
            if s_ < n:
                st1(items[s_])
            if 0 <= s_ - 1 < n:
                st2(items[s_ - 1])
            if 0 <= s_ - 2 < n:
                st3(items[s_ - 2])
            if 0 <= s_ - 3 < n:
                st4(items[s_ - 3])

    def groups(a, b, g=4):
        out = []
        t = a
        while t < b:
            n = min(g, b - t)
            out.append((t, n))
            t += n
        return out

    def run_layer(seg, l):
        NT = seg["NT"]
        x_src = seg["x_in"] if l == 0 else x1_d
        R_xsrc = seg["R_x"] if l == 0 else R_x1
        kva, kvb = seg["kv"][l]
        ata, atb = seg["at"][l]
        ffa, ffb = seg["ff"][l]
        voff = seg["voff"]
        nkv = kvb - kva
        ntok = nkv * 128
        KH = 4 if l == 0 else 12
        KCH = 4 if l == 0 else 6
        win = ws_in[l]
        nff = ffb - ffa
        ntk = nff * 128
        lay = ExitStack()
        h2T = sb("h2T", [128, 8, ntk + 2], BF16, lay)
        R_h2 = [T.res("h2T%d" % i) for i in range(nff)]
        R_h2h = T.res("h2Th")
        T.op("dve", lambda e: e.memset(h2T[:, :, 0:1], 0.0), writes=[R_h2h])
        T.op("dve", lambda e: e.memset(h2T[:, :, ntk + 1:ntk + 2], 0.0), writes=[R_h2h])
        with ExitStack() as att:
            QT = sb("QT", [128, 8, ntok], BF16, att)
            KT = sb("KT", [128, KCH, ntok], BF16, att)
            VA = sb("VA", [128, nkv, KH * 65], BF16, att)
            kmT = sb("kmT", [128, 2, 256], BF16, att)
            vma = sb("vma", [128, 2, 4 * 65], BF16, att)
            R_QT, R_KT, R_VA, R_km, R_vm = T.res("QT"), T.res("KT"), T.res("VA"), T.res("kmT"), T.res("vma")
            R_VAt = [T.res("VAt%d" % i) for i in range(nkv)]
            with ExitStack() as p1:
                hT = sb("hT", [128, 8, ntok], BF16, p1)
                R_hT = [T.res("hT%d" % i) for i in range(nkv)]
                NB1 = 4
                xsb = [sb("p1x%d" % i, [128, D], F32, p1) for i in range(NB1)]
                R_xsb = [T.res("p1x%d" % i) for i in range(NB1)]
                nbs = []
                for i in range(NB1):
                    nbs.append((sb("junk%d" % i, [128, D], BF16, p1), sb("ss%d" % i, [128, 4], F32, p1), T.res("nrm%d" % i),
                                sb("hb%d" % i, [128, D], BF16, p1), T.res("hb%d" % i)))
                memT = sb("memT", [128, 8, 256], BF16, p1)
                R_memT = T.res("memT")
                wst = [sb("wst%d" % i, [128, 8, 128], BF16, p1) for i in range(3)]
                R_wst = [T.res("wst%d" % i) for i in range(3)]
                NV = 256 if l == 0 else 768
                wv = sb("wv", [128, 8, NV], BF16, p1)
                R_wv = T.res("wv")
                wmv = sb("wmv", [128, 8, 256], BF16, p1)
                R_wmv = T.res("wmv")
                evt = [0]
                R_memTt = [T.res("memT%d" % i) for i in range(2)]
                items = []
                for mt in range(2):
                    items.append(dict(src=seg["mem"][mt * 128:(mt + 1) * 128, :], np=128, goff=GMEM[l],
                                      dst=memT[:, :, mt * 128:(mt + 1) * 128], Rdst=R_memTt[mt]))
                for k in range(nkv):
                    t = kva + k
                    items.append(dict(src=x_src[t * 128:(t + 1) * 128, :], rsrc=[R_xsrc], np=128, goff=GM[l],
                                      dst=hT[:, :, k * 128:(k + 1) * 128], Rdst=R_hT[k]))
                norm_pipeline(items, xsb, R_xsb, nbs, [0, 1])
                wi = 0

                def evac(dst, src, reads, writes, scale=None):
                    evt[0] += 1
                    if evt[0] % 2:
                        if scale is None:
                            T.op("act", lambda e: e.copy(out=dst, in_=src), reads=reads, writes=writes)
                        else:
                            T.op("act", lambda e: e.activation(out=dst, in_=src, func=AF.Copy, scale=scale), reads=reads, writes=writes)
                    else:
                        if scale is None:
                            T.op("dve", lambda e: e.tensor_copy(out=dst, in_=src), reads=reads, writes=writes)
                        else:
                            T.op("dve", lambda e: e.tensor_scalar(out=dst, in0=src, scalar1=scale, scalar2=None, op0=ALU.mult),
                                 reads=reads, writes=writes)

                pb = [4]

                def nextbank():
                    pb[0] = 4 + (pb[0] - 4 + 1) % 4
                    return bank(pb[0])

                for o in range(2):
                    w = wst[wi % 3]
                    Rw = R_wst[wi % 3]
                    wi += 1
                    T.dma("sp", w[:], ws_mk[l, o], Rw, reads=[RW[("in", l)]], writes=[Rw])
                    pa, Rp = nextbank()
                    for c in range(8):
                        T.op("pe", lambda e, c=c, w=w, pa=pa: e.matmul(pa[:, 0:256], lhsT=w[:, c, :], rhs=memT[:, c, :],
                                                                       start=(c == 0), stop=(c == 7)),
                             reads=[Rw] + R_memTt, writes=[Rp])
                    evac(kmT[:, o, :], pa[:, 0:256], [Rp], [R_km])
                T.dma("sp", wmv[:], wm_mv[l].rearrange("(c p) n -> p c n", p=128), R_wmv, reads=[RW[("in", l)]], writes=[R_wmv])
                T.op("dve", lambda e: e.memset(vma[:], 1.0), writes=[R_vm])
                for mt in range(2):
                    pa, Rp = nextbank()
                    for c in range(8):
                        T.op("pe", lambda e, c=c, pa=pa, mt=mt: e.matmul(pa[:, 0:256], lhsT=memT[:, c, mt * 128:(mt + 1) * 128],
                                                                         rhs=wmv[:, c, :], start=(c == 0), stop=(c == 7)),
                             reads=[R_wmv, R_memTt[mt]], writes=[Rp])
                    dst = vma[:, mt, :].rearrange("p (h d) -> p h d", h=4)[:, :, 0:64]
                    src = pa[:, 0:256].rearrange("p (h d) -> p h d", h=4)
                    evac(dst, src, [Rp], [R_vm])
                if l == 0:
                    chunks = [("q", o, o) for o in range(6)] + [("k", g, 6 + g) for g in range(4)] + [("q", 6 + o, 10 + o) for o in range(2)]
                else:
                    chunks = [("q", o, o) for o in range(6)] + [("k", o, 6 + o) for o in range(6)] + [("q", 6 + o, 12 + o) for o in range(2)]
                for kind, dchunk, wchunk in chunks:
                    w = wst[wi % 3]
                    Rw = R_wst[wi % 3]
                    wi += 1
                    T.dma("sp", w[:], win[wchunk], Rw, reads=[RW[("in", l)]], writes=[Rw])
                    for (g0, n) in groups(0, nkv):
                        pa, Rp = nextbank()
                        for c in range(8):
                            T.op("pe", lambda e, c=c, w=w, pa=pa, g0=g0, n=n: e.matmul(
                                pa[:, 0:n * 128], lhsT=w[:, c, :], rhs=hT[:, c, g0 * 128:(g0 + n) * 128],
                                start=(c == 0), stop=(c == 7)), reads=[Rw] + R_hT[g0:g0 + n], writes=[Rp])
                        if kind == "q":
                            evac(QT[:, dchunk, g0 * 128:(g0 + n) * 128], pa[:, 0:n * 128], [Rp], [R_QT], scale=0.125)
                        else:
                            evac(KT[:, dchunk, g0 * 128:(g0 + n) * 128], pa[:, 0:n * 128], [Rp], [R_KT])
                T.dma("sp", wv[:], wm_v[l].rearrange("(c p) n -> p c n", p=128), R_wv, reads=[RW[("in", l)]], writes=[R_wv])
                npiece = 1 if l == 0 else 2
                PW = NV // npiece
                HPP = PW // 64
                for k in range(nkv):
                    t = kva + k
                    vcol = vfl[:, voff + t:voff + t + 1]
                    for pc in range(npiece):
                        pa, Rp = nextbank()
                        for c in range(8):
                            T.op("pe", lambda e, c=c, pa=pa, k=k, pc=pc: e.matmul(
                                pa[:, 0:PW], lhsT=hT[:, c, k * 128:(k + 1) * 128], rhs=wv[:, c, pc * PW:(pc + 1) * PW],
                                start=(c == 0), stop=(c == 7)), reads=[R_wv, R_hT[k]], writes=[Rp])
                        dst = VA[:, k, pc * HPP * 65:(pc + 1) * HPP * 65].rearrange("p (h d) -> p h d", h=HPP)[:, :, 0:64]
                        src = pa[:, 0:PW].rearrange("p (h d) -> p h d", h=HPP)
                        evt[0] += 1
                        if evt[0] % 2:
                            T.op("act", lambda e, dst=dst, src=src, vcol=vcol: e.activation(
                                out=dst, in_=src, func=AF.Copy, scale=vcol), reads=[Rp, R_cl], writes=[R_VAt[k]])
                        else:
                            T.op("dve", lambda e, dst=dst, src=src, vcol=vcol: e.tensor_scalar(
                                out=dst, in0=src, scalar1=vcol, scalar2=None, op0=ALU.mult), reads=[Rp, R_cl], writes=[R_VAt[k]])
                    onesd = VA[:, k, :].rearrange("p (h d) -> p h d", h=KH)[:, :, 64:65]
                    T.op("dve", lambda e, onesd=onesd, vcol=vcol: e.tensor_copy(
                        out=onesd, in_=vcol.unsqueeze(1).broadcast_to([128, KH, 1])), reads=[R_cl], writes=[R_VAt[k]])
            T.barrier()
            with ExitStack() as p2:
                xsb = [sb("p2x%d" % i, [128, D], F32, p2) for i in range(3)]
                R_xsb = [T.res("p2x%d" % i) for i in range(3)]
                R_bt = [T.res("btab%d" % h) for h in range(12)]
                if l == 0:
                    btab = sb("alibi_hi", [128, 12 * 3 * 128], BF16, p2)
                    btab2 = sb("alibi_lo", [128, 12 * 3 * 128], BF16, p2)
                    for h in range(0, 12, 3):
                        T.dma("pool", btab[:, h * 384:(h + 3) * 384], alibi_d[:, h * 384:(h + 3) * 384], R_bt[h], writes=R_bt[h:h + 3])
                        T.dma("pool", btab2[:, h * 384:(h + 3) * 384], alibi2_d[:, h * 384:(h + 3) * 384], R_bt[h], writes=R_bt[h:h + 3])
                else:
                    btab = sb("naint_sb", [128, 12 * 5 * 128], BF16, p2)
                    for h in range(0, 12, 3):
                        T.dma("pool", btab[:, h * 640:(h + 3) * 640], naint_d[:, h * 640:(h + 3) * 640], R_bt[h], writes=R_bt[h:h + 3])
                    edg = [sb("edg%d" % i, [128, 768], BF16, p2) for i in range(4)]
                    R_edg = [T.res("edg%d" % i) for i in range(4)]
                wo = sb("wo", [128, 8, D], BF16, p2)
                R_wo = T.res("wo")
                T.dma("sp", wo[:], wm_o[l].rearrange("(c p) n -> p c n", p=128), R_wo, reads=[RW[("o", l)]], writes=[R_wo])
                NPB = 6
                PT = [sb("PT%d" % i, [128, 768], BF16, p2) for i in range(NPB)]
                R_PT = [T.res("PT%d" % i) for i in range(NPB)]
                osb = [sb("o%d" % i, [128, D], BF16, p2) for i in range(2)]
                R_o = [T.res("o%d" % i) for i in range(2)]
                oT = [sb("oT%d" % i, [128, 8, 128], BF16, p2) for i in range(2)]
                R_oT = [T.res("oT%d" % i) for i in range(2)]
                R_oTh = [[T.res("oTh%d_%d" % (i, j)) for j in range(2)] for i in range(2)]
                den = [sb("den%d" % i, [128, 32], F32, p2) for i in range(2)]
                R_den = [T.res("den%d" % i) for i in range(2)]
                LAG = 2 if l == 0 else 1
                NSC = 3 if l == 0 else 4
                EBK = [3, 4] if l == 0 else [4]
                ebk = [0]

                def next_ebank():
                    ebk[0] += 1
                    return bank(EBK[ebk[0] % len(EBK)])

                for w_ in range(24):
                    T.op("pe", lambda e: e.matmul(bank(4)[0][:, 0:512], lhsT=ident[:], rhs=QT[:, 0, 0:512], start=True, stop=True),
                         reads=[R_QT, R_const], writes=[R_ps[4]])
                n2junk = sb("n2junk", [128, D], BF16, p2)
                n2ss = [sb("n2ss%d" % i_, [128, 4], F32, p2) for i_ in range(2)]
                n2hb = [sb("n2hb%d" % i_, [128, D], BF16, p2) for i_ in range(2)]
                R_n2 = [T.res("n2_%d" % i_) for i_ in range(2)]
                R_n2hb = [T.res("n2hb%d" % i_) for i_ in range(2)]
                sbk = [0]
                ps4bf = ps_all[:, 2048:2560].bitcast(BF16)
                pend_epi = [None, None, None]
                hc = [0]
                ec = [0]
                tiles = list(range(ata, atb))
                if tiles:
                    t0 = tiles[0]
                    T.dma("sp", xsb[0][:], x_src[t0 * 128:(t0 + 1) * 128, :], R_xsb[0], reads=[R_xsrc], writes=[R_xsb[0]])
                for ti, t in enumerate(tiles):
                    i = ti % 2
                    xi = ti % 3
                    if ti + 1 < len(tiles):
                        tn = tiles[ti + 1]
                        xn = (ti + 1) % 3
                        T.dma("sp", xsb[xn][:], x_src[tn * 128:(tn + 1) * 128, :], R_xsb[xn], reads=[R_xsrc],
                              writes=[R_xsb[xn]])
                    kq = t - kva
                    pvA, R_pvA = bank(5)
                    pvB, R_pvB = bank(6)
                    pvM, R_pvM = bank(7)
                    tasks = []
                    for h in range(12):
                        tk = dict(kind="mix", h=h, edge=None)
                        if l == 0:
                            js = [j for j in (t - 1, t, t + 1) if kva <= j < kvb]
                            jrel0 = js[0] - (t - 1)
                            nk = len(js)
                            tk["bsrc"] = btab[:, (h * 3 + jrel0) * 128:(h * 3 + jrel0 + nk) * 128]
                            tk["bsrc2"] = btab2[:, (h * 3 + jrel0) * 128:(h * 3 + jrel0 + nk) * 128]
                            tk["Rb"] = R_bt[h]
                            g = h // 3
                            tk["kch"], tk["kb"], tk["vh"] = g, 64 * (h % 2), g
                        else:
                            slots = EDGE_SLOTS_P if seg["kind"] == "p" else EDGE_SLOTS_S
                            wins = EDGE_WIN_P if seg["kind"] == "p" else EDGE_WIN_S
                            if t in slots:
                                ja, jb = wins[t]
                                js = list(range(ja, jb))
                                nk = len(js)
                                si = slots.index(t)
                                if seg["kind"] == "p":
                                    tk["edge"] = edgep_d[si][:, h * 4 * 128:(h * 4 + nk) * 128]
                                else:
                                    tk["edge"] = edges_d[seg["sub"], si][:, h * 6 * 128:(h * 6 + nk) * 128]
                            else:
                                js = list(range(t - 2, t + 3))
                                nk = 5
                                tk["bsrc"] = btab[:, h * 640:(h + 1) * 640]
                                tk["bsrc2"] = None
                                tk["Rb"] = R_bt[h]
                            tk["kch"], tk["kb"], tk["vh"] = h // 2, 64 * (h % 2), h
                        for j in js:
                            assert kva <= j < kvb, (seg["kind"], l, t, j)
                        tk["js"], tk["nk"] = js, nk
                        tk["qch"], tk["qb"] = h // 2, 64 * (h % 2)
                        tk["pv"], tk["Rpv"] = (pvA, R_pvA) if h < 6 else (pvB, R_pvB)
                        tk["hh"] = h % 6
                        tasks.append(tk)
                    mixt = tasks
                    tasks = []
                    for h in range(12):
                        tasks.append(mixt[h])
                        if h in (1, 4):
                            tasks.append(dict(kind="mem", m=h // 3, nk=2))
                    tasks.append(dict(kind="mem", m=2, nk=2))
                    tasks.append(dict(kind="mem", m=3, nk=2))

                    def emitA(tk, k):
                        pi = k % NPB
                        tk["pi"] = pi
                        nk = tk["nk"]
                        subs = []
                        for i0 in range(0, nk, 4):
                            bk_, Rb_ = bank(sbk[0] % NSC)
                            sbk[0] += 1
                            subs.append((bk_, Rb_, i0, min(4, nk - i0)))
                        if tk["kind"] == "mix":
                            if tk["edge"] is not None:
                                ei = ec[0] % 4
                                ec[0] += 1
                                T.dma("pool", edg[ei][:, 0:nk * 128], tk["edge"], R_edg[ei], writes=[R_edg[ei]])
                                tk["bsrc"] = edg[ei][:, 0:nk * 128]
                                tk["bsrc2"] = None
                                tk["Rb"] = R_edg[ei]
                            kch, kb, qch, qb = tk["kch"], tk["kb"], tk["qch"], tk["qb"]
                            bsrc, bsrc2 = tk["bsrc"], tk["bsrc2"]
                            for (S, RS, i0, cnt) in subs:
                                T.op("pe", lambda e, S=S, i0=i0, cnt=cnt, bsrc=bsrc: e.matmul(
                                    S[:, 0:cnt * 128], lhsT=ident[:], rhs=bsrc[:, i0 * 128:(i0 + cnt) * 128], start=True, stop=False),
                                    reads=[tk["Rb"], R_const], writes=[RS])
                                if bsrc2 is not None:
                                    T.op("pe", lambda e, S=S, i0=i0, cnt=cnt, bsrc2=bsrc2: e.matmul(
                                        S[:, 0:cnt * 128], lhsT=ident[:], rhs=bsrc2[:, i0 * 128:(i0 + cnt) * 128], start=False, stop=False),
                                        reads=[tk["Rb"], R_const], writes=[RS])
                                for idx in range(i0, i0 + cnt):
                                    kj = tk["js"][idx] - kva
                                    T.op("pe", lambda e, S=S, idx=idx, i0=i0, cnt=cnt, kj=kj, kch=kch, kb=kb, qch=qch, qb=qb: e.matmul(
                                        S[:, (idx - i0) * 128:(idx - i0 + 1) * 128], lhsT=KT[kb:kb + 64, kch, kj * 128:(kj + 1) * 128],
                                        rhs=QT[qb:qb + 64, qch, kq * 128:(kq + 1) * 128], start=False, stop=(idx == i0 + cnt - 1)),
                                        reads=[R_KT, R_QT], writes=[RS])
                            for (S, RS, i0, cnt) in subs:
                                T.op("act", lambda e, S=S, i0=i0, cnt=cnt, pi=pi: e.activation(
                                    out=PT[pi][:, i0 * 128:(i0 + cnt) * 128], in_=S[:, 0:cnt * 128], func=AF.Exp),
                                    reads=[RS], writes=[R_PT[pi]])
                        else:
                            m = tk["m"]
                            mb = 64 * (m % 2)
                            S, RS = subs[0][0], subs[0][1]
                            for mt in range(2):
                                T.op("pe", lambda e, S=S, mt=mt, m=m, mb=mb: e.matmul(
                                    S[:, mt * 128:(mt + 1) * 128], lhsT=kmT[mb:mb + 64, m // 2, mt * 128:(mt + 1) * 128],
                                    rhs=QT[mb:mb + 64, 6 + m // 2, kq * 128:(kq + 1) * 128], start=True, stop=True),
                                    reads=[R_km, R_QT], writes=[RS])
                            T.op("act", lambda e, S=S, pi=pi: e.activation(out=PT[pi][:, 0:256], in_=S[:, 0:256], func=AF.Exp),
                                 reads=[RS], writes=[R_PT[pi]])

                    def emitB(tk):
                        pi = tk["pi"]
                        nk = tk["nk"]
                        if tk["kind"] == "mix":
                            pv, Rpv, hh, vh = tk["pv"], tk["Rpv"], tk["hh"], tk["vh"]
                            for idx, j in enumerate(tk["js"]):
                                kj = j - kva
                                T.op("pe", lambda e, pv=pv, hh=hh, idx=idx, kj=kj, vh=vh, pi=pi, nk=nk: e.matmul(
                                    pv[:, hh * 65:(hh + 1) * 65], lhsT=PT[pi][:, idx * 128:(idx + 1) * 128],
                                    rhs=VA[:, kj, vh * 65:(vh + 1) * 65], start=(idx == 0), stop=(idx == nk - 1)),
                                    reads=[R_PT[pi], R_VAt[kj]], writes=[Rpv])
                        else:
                            m = tk["m"]
                            for mt in range(2):
                                T.op("pe", lambda e, mt=mt, m=m, pi=pi: e.matmul(
                                    pvM[:, m * 65:(m + 1) * 65], lhsT=PT[pi][:, mt * 128:(mt + 1) * 128],
                                    rhs=vma[:, mt, m * 65:(m + 1) * 65], start=(mt == 0), stop=(mt == 1)),
                                    reads=[R_PT[pi], R_vm], writes=[R_pvM])

                    dn = den[i]
                    Rd = R_den[i]
                    ob = osb[i]
                    vcol = vfl[:, voff + t:voff + t + 1]

                    def emit_norm(pv, Rpv, c0, nh):
                        dsrc = pv[:, 0:nh * 65].rearrange("p (h d) -> p h d", h=nh)[:, :, 64:65]
                        ddst = dn[:, c0:c0 + nh].unsqueeze(2)
                        if l == 0 and c0 < 12:
                            T.op("dve", lambda e: e.tensor_tensor(
                                out=ddst, in0=dsrc, in1=expsink[:, c0:c0 + nh].unsqueeze(2), op=ALU.add),
                                reads=[Rpv, R_cl], writes=[Rd])
                        else:
                            T.op("dve", lambda e: e.tensor_scalar(
                                out=ddst, in0=dsrc, scalar1=1e-30, scalar2=None, op0=ALU.max), reads=[Rpv], writes=[Rd])
                        T.op("dve", lambda e: e.reciprocal(out=dn[:, 16 + c0:16 + c0 + nh], in_=dn[:, c0:c0 + nh]), reads=[Rd], writes=[Rd])
                        src = pv[:, 0:nh * 65].rearrange("p (h d) -> p h d", h=nh)[:, :, 0:64]
                        dst = ob[:, c0 * 64:(c0 + nh) * 64].rearrange("p (h d) -> p h d", h=nh)
                        rb = dn[:, 16 + c0:16 + c0 + nh].unsqueeze(2).broadcast_to([128, nh, 64])
                        T.op("dve", lambda e: e.scalar_tensor_tensor(out=dst, in0=src, scalar=vcol, in1=rb, op0=ALU.mult, op1=ALU.mult),
                             reads=[Rpv, Rd, R_cl], writes=[R_o[i]])

                    pending = {"A": 6, "B": 6, "M": 4}

                    def doB(tk):
                        emitB(tk)
                        g = "M" if tk["kind"] == "mem" else ("A" if tk["h"] < 6 else "B")
                        pending[g] -= 1
                        if pending[g] == 0:
                            if g == "A":
                                emit_norm(pvA, R_pvA, 0, 6)
                            elif g == "B":
                                emit_norm(pvB, R_pvB, 6, 6)
                            else:
                                emit_norm(pvM, R_pvM, 12, 4)

                    def make_epi(t=t, i=i, xi=xi, ob=ob):
                        def e1(part):
                            pa, Rp = next_ebank()
                            for c in range(4 * part, 4 * part + 4):
                                T.op("pe", lambda e, c=c: e.matmul(pa[:, (c % 4) * 128:(c % 4 + 1) * 128], lhsT=ob[:, c * 128:(c + 1) * 128],
                                                                   rhs=ident[:], start=True, stop=True),
                                     reads=[R_o[i], R_const], writes=[Rp])
                            if part == 0:
                                T.op("act", lambda e: e.copy(out=oT[i][:, 0:4, :], in_=pa[:, 0:512].rearrange("p (c n) -> p c n", c=4)),
                                     reads=[Rp], writes=[R_oTh[i][0]])
                            else:
                                T.op("dve", lambda e: e.tensor_copy(out=oT[i][:, 4:8, :], in_=pa[:, 0:512].rearrange("p (c n) -> p c n", c=4)),
                                     reads=[Rp], writes=[R_oTh[i][1]])

                        def ehalf(half):
                            pa, Rp = next_ebank()
                            for c in range(8):
                                T.op("pe", lambda e, c=c: e.matmul(
                                    pa[:, 0:512], lhsT=oT[i][:, c, :], rhs=wo[:, c, half * 512:(half + 1) * 512],
                                    start=(c == 0), stop=(c == 7)), reads=[R_oTh[i][c // 4], R_wo], writes=[Rp])
                            T.op("dve", lambda e: e.tensor_tensor(
                                out=xsb[xi][:, half * 512:(half + 1) * 512], in0=xsb[xi][:, half * 512:(half + 1) * 512],
                                in1=pa[:, 0:512], op=ALU.add), reads=[Rp, R_xsb[xi]], writes=[R_xsb[xi]])

                        def e3():
                            ehalf(1)
                            T.dma("sp", xm_d[t * 128:(t + 1) * 128, :], xsb[xi][:], R_xsb[xi], reads=[R_xsb[xi]], writes=[R_xm])
                        need = (ffa - 1 <= t <= ffb)

                        def e_norm():
                            if not need:
                                return
                            ss, hb, Rn, Rhb = n2ss[i], n2hb[i], R_n2[i], R_n2hb[i]
                            xs_ap = xsb[xi][:]
                            T.op("act", lambda e: e.activation(out=n2junk[:], in_=xs_ap, func=AF.Square, accum_out=ss[:, 0:1]),
                                 reads=[R_xsb[xi]], writes=[Rn])
                            T.op("act", lambda e: e.activation(out=ss[:, 2:3], in_=ss[:, 0:1], func=AF.Ln, scale=1.0 / D, bias=epsc[:, 0:1]),
                                 reads=[Rn, R_const], writes=[Rn])
                            T.op("act", lambda e: e.activation(out=ss[:, 3:4], in_=ss[:, 2:3], func=AF.Exp, scale=-0.5), reads=[Rn], writes=[Rn])
                            if i == 0:
                                T.op("act", lambda e: e.activation(out=hb[:], in_=xs_ap, func=AF.Copy, scale=ss[:, 3:4]),
                                     reads=[R_xsb[xi], Rn], writes=[Rhb])
                            else:
                                T.op("dve", lambda e: e.tensor_scalar(out=hb[:], in0=xs_ap, scalar1=ss[:, 3:4], scalar2=None, op0=ALU.mult),
                                     reads=[R_xsb[xi], Rn], writes=[Rhb])

                        def e_t(part):
                            if not need:
                                return
                            hb, Rhb = n2hb[i], R_n2hb[i]
                            pa, Rp = next_ebank()
                            for c in range(4 * part, 4 * part + 4):
                                T.op("pe", lambda e, c=c: e.matmul(pa[:, (c % 4) * 128:(c % 4 + 1) * 128], lhsT=hb[:, c * 128:(c + 1) * 128],
                                                                   rhs=ident[:], start=True, stop=True),
                                     reads=[Rhb, R_const], writes=[Rp])
                            src = pa[:, 0:512].rearrange("p (c n) -> p c n", c=4)
                            goff = GFFN[l] + 4 * part
                            if ffa <= t < ffb:
                                k_ = t - ffa
                                dst = h2T[:, 4 * part:4 * part + 4, 1 + k_ * 128:1 + (k_ + 1) * 128]
                                gb = gcols[:, goff:goff + 4].unsqueeze(2).broadcast_to([128, 4, 128])
                                T.op("dve", lambda e: e.tensor_tensor(out=dst, in0=src, in1=gb, op=ALU.mult),
                                     reads=[Rp, R_cl], writes=[R_h2[k_]])
                            else:
                                col = 127 if t == ffa - 1 else 0
                                dcol = 0 if t == ffa - 1 else ntk + 1
                                dst = h2T[:, 4 * part:4 * part + 4, dcol:dcol + 1]
                                gb = gcols[:, goff:goff + 4].unsqueeze(2)
                                T.op("dve", lambda e: e.tensor_tensor(out=dst, in0=src[:, :, col:col + 1], in1=gb, op=ALU.mult),
                                     reads=[Rp, R_cl], writes=[R_h2h])

                        def e3n():
                            e3()
                            e_norm()
                        return [lambda: e1(0), lambda: e1(1), lambda: ehalf(0), e3n], [lambda: e_t(0), lambda: e_t(1)]

                    prev = pend_epi[0]
                    prevt = pend_epi[1]
                    for k, tk in enumerate(tasks):
                        emitA(tk, hc[0])
                        hc[0] += 1
                        if prev is not None:
                            if k == 1:
                                prev[0]()
                            elif k == 3:
                                prev[1]()
                            elif k == 5:
                                prev[2]()
                                if l == 0:
                                    prev[3]()
                            elif k == 8 and l == 1:
                                prev[3]()
                        if prevt is not None:
                            if k == 2:
                                prevt[0]()
                            elif k == 4:
                                prevt[1]()
                        if k >= LAG:
                            doB(tasks[k - LAG])
                    for k in range(max(0, len(tasks) - LAG), len(tasks)):
                        doB(tasks[k])
                    pend_epi[1] = pend_epi[2]
                    pend_epi[0], pend_epi[2] = make_epi()
                if pend_epi[1] is not None:
                    for f_ in pend_epi[1]:
                        f_()
                if pend_epi[0] is not None:
                    for f_ in pend_epi[0]:
                        f_()
                    for f_ in pend_epi[2]:
                        f_()
            T.barrier()
        with ExitStack() as p3:
            aT = sb("aT", [128, NFC, ntk], BF16, p3)
            R_aT = T.res("aT")
            p3b = p3
            wd = sb("wd", [128, NFC, D], BF16, p3b)
            R_wdc = [T.res("wd%d" % c) for c in range(NFC // 2)]

            def load_wd(cp):
                c0 = 2 * cp
                T.dma("sp", wd[:, c0:c0 + 2, :], wm_d[l][c0 * 128:(c0 + 2) * 128, :].rearrange("(c p) n -> p c n", p=128),
                      R_wdc[cp], reads=[RW[("d", l)]], writes=[R_wdc[cp]])
            xsb = [sb("p4x%d" % i, [128, D], F32, p3b) for i in range(3)]
            R_xsb = [T.res("p4x%d" % i) for i in range(3)]
            if l == 1:
                gfin = sb("gfin_sb", [128, D], F32, p3b)
                R_gf = T.res("gfin")
                T.dma("sp", gfin[:], gfin_d[:, :], R_gf, writes=[R_gf])
                ysb1 = sb("y0", [128, D], F32, p3b)
                ysb = [ysb1, ysb1]
                R_y1 = T.res("y0")
                R_y = [R_y1, R_y1]
                junk = sb("junkf", [128, D], BF16, p3b)
                ssf = [sb("ssf%d" % i, [128, 4], F32, p3b) for i in range(2)]
                R_nf = [T.res("nf%d" % i) for i in range(2)]
            NWB = 3 if l == 0 else 2
            with ExitStack() as p3a:
                wg = [sb("wg%d" % i, [128, 8, 128], BF16, p3a) for i in range(NWB)]
                wu = [sb("wu%d" % i, [128, 8, 128], BF16, p3a) for i in range(NWB)]
                R_wg = [T.res("wg%d" % i) for i in range(NWB)]
                R_wu = [T.res("wu%d" % i) for i in range(NWB)]
                gbuf = [sb("gbuf%d" % i, [128, 514], F32, p3a) for i in range(2)]
                R_gb = [T.res("gbuf%d" % i) for i in range(2)]
                cv = [sb("cv%d" % i, [128, 512], F32, p3a) for i in range(2)]
                R_cv = [T.res("cv%d" % i) for i in range(2)]
                sg = cv
                R_sg = R_cv
                it = 0
                grp = groups(0, nff)

                def loadw(c):
                    T.dma("sp", wg[c % NWB][:], ws_g[l, c], R_wg[c % NWB], reads=[RW[("gu", l)]], writes=[R_wg[c % NWB]])
                    T.dma("sp", wu[c % NWB][:], ws_u[l, c], R_wu[c % NWB], reads=[RW[("gu", l)]], writes=[R_wu[c % NWB]])

                for c_ in range(NWB - 1):
                    loadw(c_)
                for c in range(NFC):
                    if c + NWB - 1 < NFC:
                        loadw(c + NWB - 1)
                    if c < NFC // 2:
                        load_wd(c)
                    cw = convp[:, (l * NFC + c) * 4:(l * NFC + c) * 4 + 4]
                    for (g0, n) in grp:
                        b = it % 2
                        it += 1
                        n128 = n * 128
                        c0 = g0 * 128
                        pg, Rpg = bank(0 + b)
                        pt, Rpt = bank(2)
                        pu, Rpu = bank(3 + b)
                        hres = R_h2[g0:g0 + n] + [R_h2h] + ([R_h2[g0 - 1]] if g0 > 0 else []) + ([R_h2[g0 + n]] if g0 + n < nff else [])
                        for ci in range(8):
                            T.op("pe", lambda e, ci=ci, pg=pg, c=c, c0=c0, n128=n128: e.matmul(
                                pg[:, 0:n128], lhsT=wg[c % NWB][:, ci, :], rhs=h2T[:, ci, c0:c0 + n128],
                                start=(ci == 0), stop=(ci == 7)), reads=[R_wg[c % NWB]] + hres, writes=[Rpg])
                        for ci in range(8):
                            T.op("pe", lambda e, ci=ci, pt=pt, b=b, c=c, c0=c0, n128=n128: e.matmul(
                                pt[:, 2 * b:2 * b + 2], lhsT=wg[c % NWB][:, ci, :], rhs=h2T[:, ci, c0 + n128:c0 + n128 + 2],
                                start=(ci == 0), stop=(ci == 7)), reads=[R_wg[c % NWB]] + hres, writes=[Rpt])
                        for ci in range(8):
                            T.op("pe", lambda e, ci=ci, pu=pu, c=c, c0=c0, n128=n128: e.matmul(
                                pu[:, 0:n128], lhsT=wu[c % NWB][:, ci, :], rhs=h2T[:, ci, c0 + 1:c0 + 1 + n128],
                                start=(ci == 0), stop=(ci == 7)), reads=[R_wu[c % NWB]] + hres, writes=[Rpu])
                        gb_ = gbuf[b]
                        T.op("act", lambda e, gb_=gb_, pg=pg, n128=n128: e.copy(out=gb_[:, 0:n128], in_=pg[:, 0:n128]),
                             reads=[Rpg], writes=[R_gb[b]])
                        T.op("act", lambda e, gb_=gb_, pt=pt, b=b, n128=n128: e.copy(out=gb_[:, n128:n128 + 2], in_=pt[:, 2 * b:2 * b + 2]),
                             reads=[Rpt], writes=[R_gb[b]])
                        cvb = cv[b]
                        T.op("dve", lambda e, cvb=cvb, gb_=gb_, cw=cw, n128=n128: e.tensor_scalar(
                            out=cvb[:, 0:n128], in0=gb_[:, 0:n128], scalar1=cw[:, 0:1], scalar2=None, op0=ALU.mult),
                            reads=[R_gb[b], R_cl], writes=[R_cv[b]])
                        T.op("dve", lambda e, cvb=cvb, gb_=gb_, cw=cw, n128=n128: e.scalar_tensor_tensor(
                            out=cvb[:, 0:n128], in0=gb_[:, 1:1 + n128], scalar=cw[:, 1:2], in1=cvb[:, 0:n128],
                            op0=ALU.mult, op1=ALU.add), reads=[R_gb[b], R_cl, R_cv[b]], writes=[R_cv[b]])
                        T.op("dve", lambda e, cvb=cvb, gb_=gb_, cw=cw, n128=n128: e.scalar_tensor_tensor(
                            out=cvb[:, 0:n128], in0=gb_[:, 2:2 + n128], scalar=cw[:, 2:3], in1=cvb[:, 0:n128],
                            op0=ALU.mult, op1=ALU.add), reads=[R_gb[b], R_cl, R_cv[b]], writes=[R_cv[b]])
                        sgb = sg[b]
                        T.op("act", lambda e, sgb=sgb, cvb=cvb, cw=cw, n128=n128: e.activation(
                            out=sgb[:, 0:n128], in_=cvb[:, 0:n128], func=AF.Silu, bias=cw[:, 3:4]),
                            reads=[R_cv[b], R_cl], writes=[R_sg[b]])
                        T.op("dve", lambda e, sgb=sgb, pu=pu, c=c, g0=g0, n128=n128: e.tensor_tensor(
                            out=aT[:, c, g0 * 128:g0 * 128 + n128], in0=sgb[:, 0:n128], in1=pu[:, 0:n128], op=ALU.mult),
                            reads=[R_sg[b], Rpu], writes=[R_aT])
            if True:
                def p4load(k):
                    if k < nff:
                        T.dma("sp", xsb[k % 3][:], xm_d[(ffa + k) * 128:(ffa + k + 1) * 128, :], R_xsb[k % 3], reads=[R_xm],
                              writes=[R_xsb[k % 3]])

                p4load(0)
                p4load(1)
                NG0 = min(4, nff)
                for c in range(NFC):
                    for k in range(NG0):
                        for half in range(2):
                            pa, Rp = bank((k % 4) * 2 + half)
                            T.op("pe", lambda e, c=c, pa=pa, half=half, k=k: e.matmul(
                                pa[:, 0:512], lhsT=aT[:, c, k * 128:(k + 1) * 128], rhs=wd[:, c, half * 512:(half + 1) * 512],
                                start=(c == 0), stop=(c == NFC - 1)), reads=[R_aT, R_wdc[c // 2]], writes=[Rp])
                for k in range(nff):
                    t = ffa + k
                    i = k % 3
                    p4load(k + 2)
                    for half in range(2):
                        pa, Rp = bank((k % 4) * 2 + half)
                        for c in range(NFC if k >= NG0 else 0):
                            T.op("pe", lambda e, c=c, pa=pa, half=half, k=k: e.matmul(
                                pa[:, 0:512], lhsT=aT[:, c, k * 128:(k + 1) * 128], rhs=wd[:, c, half * 512:(half + 1) * 512],
                                start=(c == 0), stop=(c == NFC - 1)), reads=[R_aT, R_wdc[c // 2]], writes=[Rp])
                        T.op("dve", lambda e, pa=pa, half=half, i=i: e.tensor_tensor(
                            out=xsb[i][:, half * 512:(half + 1) * 512], in0=xsb[i][:, half * 512:(half + 1) * 512],
                            in1=pa[:, 0:512], op=ALU.add), reads=[Rp, R_xsb[i]], writes=[R_xsb[i]])
                    if l == 0:
                        T.dma("sp", x1_d[t * 128:(t + 1) * 128, :], xsb[i][:], R_xsb[i], reads=[R_xsb[i]], writes=[R_x1])
                    else:
                        j = k % 2
                        rstd, Rn = rms_rstd(None, xsb[i][:], 128, "", R_xsb[i], (junk, ssf[j], R_nf[j]))
                        T.op("dve", lambda e, i=i, j=j, rstd=rstd: e.scalar_tensor_tensor(
                            out=ysb[j][:], in0=xsb[i][:], scalar=rstd, in1=gfin[:], op0=ALU.mult, op1=ALU.mult),
                            reads=[R_xsb[i], Rn, R_gf], writes=[R_y[j]])
                        oa, ob_ = seg["own"]
                        if oa <= t < ob_:
                            T.dma("sp", seg["y_out"][(t - oa) * 128:(t - oa + 1) * 128, :], ysb[j][:], R_y[j],
                                  reads=[R_y[j]], writes=[seg["R_y"]])
            T.barrier()
        lay.close()

    segs = []
    for s in range(4):
        segs.append(dict(kind="p", NT=NT_P, x_in=xp[s], R_x=T.res("xin"), mem=memp[s], voff=0,
                         kv=[(0, 16), (0, 16)], at=[(0, 16), (0, 16)], ff=[(0, 16), (0, 16)], own=(0, 16),
                         y_out=yp[s], R_y=T.res("yout")))
    for j in range(2):
        segs.append(dict(kind="s", sub=j, NT=NT_S, x_in=xs_in[j], R_x=T.res("xin"), mem=mems, voff=NT_S * (1 + j),
                         kv=[(0, 18), (1, 17)], at=[(1, 17), (3, 15)], ff=[(1, 17), (5, 13)], own=(5, 13),
                         y_out=ys[j], R_y=T.res("yout")))
    import os
    nseg = int(os.environ.get("MK_NSEG", "6"))
    for seg in segs[:nseg] if nseg > 0 else segs[4:4 - nseg]:
        for l in range(2):
            run_layer(seg, l)
    T.barrier()
    st.close()
    return nc, T.ninst


_PROG = None


def kernel(x_prompt, x_sample, mem_prompt, mem_sample, g_mix, g_mem, w_in_a, sink_a, w_in_b, rpb_b,
           w_mem_kv, w_o, g_ffn, w_gate, w_up, conv_w, conv_b, w_down, g_final):
    global _PROG
    f = lambda a: np.ascontiguousarray(np.asarray(a, dtype=np.float32))
    x_prompt, x_sample, mem_prompt, mem_sample = f(x_prompt), f(x_sample), f(mem_prompt), f(mem_sample)
    g_mix, g_mem, g_ffn, g_final = f(g_mix), f(g_mem), f(g_ffn), f(g_final)
    conv_w, conv_b, sink_a, rpb_b = f(conv_w), f(conv_b), f(sink_a), f(rpb_b)
    if _PROG is None:
        _PROG = build_program()
    nc, _ = _PROG

    def gcol(v):
        return v.reshape(8, 128).T
    gcols = np.concatenate([gcol(g_mix[0]), gcol(g_mix[1]), gcol(g_mem[0]), gcol(g_mem[1]),
                            gcol(g_ffn[0]), gcol(g_ffn[1]), gcol(g_final)], axis=1)
    gcols = np.ascontiguousarray(gcols, dtype=np.float32)
    gfin = np.ascontiguousarray(np.broadcast_to(g_final[None, :], (128, D)))
    cp = np.concatenate([conv_w, conv_b[:, None, :]], axis=1)
    cp = cp.reshape(2, 4, NFC, 128).transpose(3, 0, 2, 1)
    convp = np.ascontiguousarray(cp.reshape(128, 2 * NFC * 4))
    sinkb = np.ascontiguousarray(np.broadcast_to(sink_a[0][None, :], (128, 12)))
    import ml_dtypes
    alibi_full = _alibi_table().reshape(128, -1)
    alibi = alibi_full.astype(ml_dtypes.bfloat16).astype(np.float32)
    alibi2 = (alibi_full - alibi).astype(np.float32)
    rpb_pad = np.concatenate([rpb_b[0].reshape(12, 15 * 31), np.full((12, 1), NEG, np.float32)], axis=1)
    naint = _gather_na(rpb_pad, _na_index(100, [98, 99, 100, 101, 102], 1000)).reshape(128, -1)
    edgep = np.zeros((4, 128, 12, 4, 128), np.float32)
    for si, t in enumerate(EDGE_SLOTS_P):
        ja, jb = EDGE_WIN_P[t]
        edgep[si] = _gather_na(rpb_pad, _na_index(t, list(range(ja, jb)), 32))
    edgep = edgep.reshape(4, 128, -1)
    SR = 256
    in_maps = []
    for c in range(NCORES):
        xs = np.zeros((2, NT_S * 128, D), np.float32)
        vf = np.ones((128, 3 * NT_S), np.float32)
        edges = np.full((2, 4, 128, 12, 6, 128), NEG, np.float32)
        for j in range(2):
            g0 = 16 * c + 8 * j - S_OFF
            for lt in range(NT_S):
                gt = g0 + lt
                if 0 <= gt < 128:
                    xs[j, lt * 128:(lt + 1) * 128] = x_sample[0, gt * 128:(gt + 1) * 128]
                else:
                    vf[:, NT_S * (1 + j) + lt] = 0.0
            for si, lt in enumerate(EDGE_SLOTS_S):
                ja, jb = EDGE_WIN_S[lt]
                gks = [g0 + jj for jj in range(ja, jb)]
                edges[j, si, :, :, 0:len(gks), :] = _gather_na(rpb_pad, _na_index(g0 + lt, gks, SR))
        in_maps.append({
            "xp": np.ascontiguousarray(x_prompt[4 * c:4 * c + 4]),
            "xs": xs,
            "memp": np.ascontiguousarray(mem_prompt[4 * c:4 * c + 4]),
            "mems": np.ascontiguousarray(mem_sample[0]),
            "vflag": vf, "gcols": gcols, "gfin": gfin, "convp": convp, "sinkb": sinkb,
            "alibi": alibi, "alibi2": alibi2, "naint": naint, "edgep": edgep,
            "edges": np.ascontiguousarray(edges.reshape(2, 4, 128, -1)),
            "w_in_a": f(w_in_a)[0], "w_in_b": f(w_in_b)[0], "w_mem_kv": f(w_mem_kv), "w_o": f(w_o),
            "w_gate": f(w_gate), "w_up": f(w_up), "w_down": f(w_down),
        })
    res = run_bass_kernel_spmd(nc, in_maps, core_ids=list(range(NCORES)))
    y_prompt = np.concatenate([r["yp"] for r in res.results], axis=0).astype(np.float32)
    y_sample = np.concatenate([r["ys"].reshape(2048, D) for r in res.results], axis=0)[None].astype(np.float32)
    return (y_prompt, y_sample)
```

```python
import numpy as np
from contextlib import ExitStack
import concourse.bass as bass
import concourse.mybir as mybir
from concourse.bass_utils import run_bass_kernel_spmd

F32, BF16 = mybir.dt.float32, mybir.dt.bfloat16
AF = mybir.ActivationFunctionType
ALU = mybir.AluOpType
AX = mybir.AxisListType
NEG = -30000.0
NCORES = 8
D = 1024
DFF = 2816
NFC = 22
EPS = 1e-6
NT_P, NT_S = 16, 18
S_OFF = 5
EDGE_SLOTS_S = (5, 6, 11, 12)
EDGE_WIN_S = {5: (3, 9), 6: (4, 9), 11: (9, 14), 12: (9, 15)}
EDGE_SLOTS_P = (0, 1, 14, 15)
EDGE_WIN_P = {0: (0, 4), 1: (0, 4), 14: (12, 16), 15: (12, 16)}


class Res:
    __slots__ = ("name", "w", "rs", "sem", "cnt")

    def __init__(self, name):
        self.name = name
        self.w = None
        self.rs = []
        self.sem = None
        self.cnt = 0


class Tr:
    def __init__(self, nc, stack):
        self.nc = nc
        self.stack = stack
        self.engs = {"pe": nc.tensor, "act": nc.scalar, "dve": nc.vector, "pool": nc.gpsimd, "sp": nc.sync}
        self.sem = {k: stack.enter_context(nc.semaphore("sem_" + k)) for k in self.engs}
        self.cnt = {k: 0 for k in self.engs}
        self.waited = {k: {} for k in self.engs}
        self.dirty = {}
        self.dsem = {}
        self.ninst = 0

    def res(self, name):
        return Res(name)

    def _wait(self, eng, sem, val):
        w = self.waited[eng]
        k = id(sem)
        if w.get(k, 0) >= val:
            return
        w[k] = val
        self.engs[eng].wait_ge(sem, val)
        self.ninst += 1

    def _deps(self, eng, reads, writes, is_dma):
        need = {}

        def add(tok, kind):
            sem, val, teng = tok
            if (not is_dma) and teng == eng:
                if eng == "pe" or kind != "raw":
                    return
            k = id(sem)
            if k not in need or need[k][1] < val:
                need[k] = (sem, val)

        for r in reads:
            if r.w is not None:
                add(r.w, "raw")
        for w in writes:
            if w.w is not None:
                add(w.w, "waw")
            for t in w.rs:
                add(t, "war")
        for sem, val in need.values():
            self._wait(eng, sem, val)

    def _commit(self, tok, reads, writes):
        for r in reads:
            rs = [t for t in r.rs if t[0] is not tok[0]]
            rs.append(tok)
            r.rs = rs
            self.dirty[id(r)] = r
        for w in writes:
            w.w = tok
            w.rs = []
            self.dirty[id(w)] = w

    def op(self, eng, fn, reads=(), writes=()):
        self._deps(eng, reads, writes, False)
        ins = fn(self.engs[eng])
        self.cnt[eng] += 1
        ins.then_inc(self.sem[eng], 1)
        self.ninst += 1
        self._commit((self.sem[eng], self.cnt[eng], eng), reads, writes)

    def dma(self, q, out, in_, owner, reads=(), writes=(), nowaw=False, **kw):
        self._deps(q, reads, () if nowaw else writes, True)
        h = self.dsem.get(owner.name)
        if h is None:
            h = [self.stack.enter_context(self.nc.semaphore("dsem_%s" % owner.name)), 0]
            self.dsem[owner.name] = h
        ins = self.engs[q].dma_start(out=out, in_=in_, **kw)
        h[1] += 16
        ins.then_inc(h[0], 16)
        self.ninst += 1
        self._commit((h[0], h[1], None), reads, writes)

    def barrier(self):
        for e in self.engs:
            for f in self.engs:
                if self.cnt[f] > 0:
                    self._wait(e, self.sem[f], self.cnt[f])
            for nm, h in self.dsem.items():
                if not nm.startswith("wcast"):
                    self._wait(e, h[0], h[1])
        keep = {}
        for k, r in self.dirty.items():
            if r.name.startswith("wcast"):
                r.rs = []
                keep[k] = r
                continue
            r.w = None
            r.rs = []
        self.dirty = keep


def _alibi_table():
    slopes = (2.0 ** (-8.0 * np.arange(1, 13, dtype=np.float32) / 12)).astype(np.float32)
    p = np.arange(128)[:, None, None]
    j = np.arange(3)[None, :, None] - 1
    q = np.arange(128)[None, None, :]
    rel = q - (p + 128 * j)
    a = np.abs(rel).astype(np.float32)
    tab = -slopes[None, :, None, None] * a[:, None, :, :]
    tab = np.where((a <= 128)[:, None, :, :], tab, np.float32(NEG)).astype(np.float32)
    return np.ascontiguousarray(tab)


def _na_index(gq, gks, R):
    p = np.arange(128)
    krl, kc = p // 64, p % 64
    q = np.arange(128)
    qrl, qc = q // 64, q % 64
    out = np.full((128, len(gks), 128), 465, dtype=np.int64)
    r = 2 * gq + qrl
    r0 = np.clip(r - 4, 0, R - 8)
    c0 = np.clip(qc - 8, 0, 48)
    for ji, gk in enumerate(gks):
        kr = 2 * gk + krl
        vr = (kr[:, None] >= r0[None, :]) & (kr[:, None] < r0[None, :] + 8) & (kr[:, None] >= 0) & (kr[:, None] < R)
        vc = (kc[:, None] >= c0[None, :]) & (kc[:, None] < c0[None, :] + 16)
        dr = kr[:, None] - r[None, :] + 7
        dc = kc[:, None] - qc[None, :] + 15
        ok = vr & vc & (dr >= 0) & (dr < 15) & (dc >= 0) & (dc < 31)
        idx = np.where(ok, dr * 31 + dc, 465)
        out[:, ji, :] = idx
    return out


def _gather_na(rpb_pad, idx):
    t = rpb_pad[:, idx]
    return np.ascontiguousarray(np.transpose(t, (1, 0, 2, 3)))


def build_program():
    nc = bass.Bass("TRN2", target_bir_lowering=False)
    st = ExitStack()
    T = Tr(nc, st)

    def din(name, shape, dt=F32):
        return nc.dram_tensor(name, list(shape), dt, kind="ExternalInput").ap()

    xp = din("xp", [4, NT_P * 128, D])
    xs_in = din("xs", [2, NT_S * 128, D])
    memp = din("memp", [4, 256, D])
    mems = din("mems", [256, D])
    vflag = din("vflag", [128, 3 * NT_S])
    gcols_d = din("gcols", [128, 7 * 8])
    gfin_d = din("gfin", [128, D])
    convp_d = din("convp", [128, 2 * NFC * 4])
    sink_d = din("sinkb", [128, 12])
    alibi_d = din("alibi", [128, 12 * 3 * 128])
    alibi2_d = din("alibi2", [128, 12 * 3 * 128])
    naint_d = din("naint", [128, 12 * 5 * 128])
    edgep_d = din("edgep", [4, 128, 12 * 4 * 128])
    edges_d = din("edges", [2, 4, 128, 12 * 6 * 128])
    w_in_a = din("w_in_a", [D, 1536])
    w_in_b = din("w_in_b", [D, 2560])
    w_mkv = din("w_mem_kv", [2, D, 512])
    w_o = din("w_o", [2, D, D])
    w_g = din("w_gate", [2, D, DFF])
    w_u = din("w_up", [2, D, DFF])
    w_d = din("w_down", [2, DFF, D])
    yp = nc.dram_tensor("yp", [4, NT_P * 128, D], F32, kind="ExternalOutput").ap()
    ys = nc.dram_tensor("ys", [2, 8 * 128, D], F32, kind="ExternalOutput").ap()

    def dscr(name, shape, dt=BF16):
        return nc.dram_tensor(name, list(shape), dt).ap()

    ws_in = [dscr("ws_in0", [6 + 4 + 2, 128, 8, 128]), dscr("ws_in1", [6 + 6 + 2, 128, 8, 128])]
    ws_mk = dscr("ws_mk", [2, 2, 128, 8, 128])
    ws_g = dscr("ws_g", [2, NFC, 128, 8, 128])
    ws_u = dscr("ws_u", [2, NFC, 128, 8, 128])
    wm_v = [dscr("wm_v0", [D, 256]), dscr("wm_v1", [D, 768])]
    wm_mv = dscr("wm_mv", [2, D, 256])
    wm_o = dscr("wm_o", [2, D, D])
    wm_d = dscr("wm_d", [2, DFF, D])
    xm_d = dscr("xm_d", [NT_S * 128, D], F32)
    x1_d = dscr("x1_d", [NT_S * 128, D], F32)
    R_xm, R_x1 = T.res("xm_d"), T.res("x1_d")
    R_w = T.res("wscr")

    RW = {(k, l): T.res("wcast_%s%d" % (k, l)) for k in ("in", "o", "gu", "d") for l in range(2)}

    def cast_stat(dst, W, col0, R):
        src = W[:, col0:col0 + 128].rearrange("(c p) n -> p c n", p=128)
        T.dma("pool", dst, src, R, writes=[R], nowaw=True)

    def cast_stat_dup(dst, W, col0, R):
        src = W[:, col0:col0 + 64].rearrange("(c p) n -> p c n", p=128)
        T.dma("pool", dst[:, :, 0:64], src, R, writes=[R], nowaw=True)
        T.dma("pool", dst[:, :, 64:128], src, R, writes=[R], nowaw=True)

    def cast_mov(dst, W, rows, R):
        for r0 in range(0, rows, 512):
            r1 = min(rows, r0 + 512)
            T.dma("pool", dst[r0:r1, :], W[r0:r1, :], R, writes=[R], nowaw=True)

    def prologue_casts():
        for l in range(2):
            R = RW[("in", l)]
            if l == 0:
                wa = w_in_a
                for o in range(6):
                    cast_stat(ws_in[0][o], wa, o * 128, R)
                for g in range(4):
                    cast_stat_dup(ws_in[0][6 + g], wa, 768 + g * 64, R)
                for o in range(2):
                    cast_stat(ws_in[0][10 + o], wa, 1280 + o * 128, R)
                cast_mov(wm_v[0], wa[:, 1024:1280], D, R)
            else:
                wb = w_in_b
                for o in range(12):
                    cast_stat(ws_in[1][o], wb, o * 128, R)
                for o in range(2):
                    cast_stat(ws_in[1][12 + o], wb, 2304 + o * 128, R)
                cast_mov(wm_v[1], wb[:, 1536:2304], D, R)
            for o in range(2):
                cast_stat(ws_mk[l, o], w_mkv[l], o * 128, R)
            cast_mov(wm_mv[l], w_mkv[l][:, 256:512], D, R)
            cast_mov(wm_o[l], w_o[l], D, RW[("o", l)])
            for c in range(NFC):
                cast_stat(ws_g[l, c], w_g[l], c * 128, RW[("gu", l)])
                cast_stat(ws_u[l, c], w_u[l], c * 128, RW[("gu", l)])
            cast_mov(wm_d[l], w_d[l], DFF, RW[("d", l)])

    _uid = [0]

    def sb(name, shape, dt, stack=st):
        _uid[0] += 1
        return stack.enter_context(nc.sbuf_tensor("%s_u%d" % (name, _uid[0]), list(shape), dt))

    ident = sb("ident", [128, 128], BF16)
    identf = sb("identf", [128, 128], F32)
    gcols = sb("gcols_sb", [128, 56], F32)
    convp = sb("convp_sb", [128, 2 * NFC * 4], F32)
    expsink = sb("expsink", [128, 12], F32)
    vfl = sb("vfl", [128, 3 * NT_S], F32)
    R_const = T.res("const")
    T.op("pool", lambda e: e.memset(identf[:], 1.0), writes=[R_const])
    T.op("pool", lambda e: e.affine_select(out=identf[:], in_=identf[:], pattern=[[-1, 128]],
                                           compare_op=ALU.is_equal, fill=0.0, base=0, channel_multiplier=1),
         reads=[R_const], writes=[R_const])
    T.op("pool", lambda e: e.tensor_copy(out=ident[:], in_=identf[:]), reads=[R_const], writes=[R_const])
    epsc = sb("epsc", [128, 1], F32)
    T.op("pool", lambda e: e.memset(epsc[:], EPS), writes=[R_const])
    R_cl = T.res("constload")
    T.dma("sp", gcols[:], gcols_d[:, :], R_cl, writes=[R_cl])
    T.dma("sp", convp[:], convp_d[:, :], R_cl, writes=[R_cl])
    T.dma("sp", expsink[:], sink_d[:, :], R_cl, writes=[R_cl])
    T.dma("sp", vfl[:], vflag[:, :], R_cl, writes=[R_cl])
    T.op("act", lambda e: e.activation(out=expsink[:], in_=expsink[:], func=AF.Exp), reads=[R_cl], writes=[R_cl])

    ps_all = st.enter_context(nc.psum_tensor("ps_all", [128, 4096], F32))
    psq = [ps_all[:, i * 1024:(i + 1) * 1024] for i in range(4)]
    R_ps = [T.res("psbank%d" % i) for i in range(8)]

    def bank(i):
        return psq[i // 2][:, (i % 2) * 512:(i % 2) * 512 + 512], R_ps[i]

    prologue_casts()

    GM = [0, 8]
    GMEM = [16, 24]
    GFFN = [32, 40]

    def rms_rstd(ph, xs_ap, np_, tag, Rx, bufs):
        junk, ss, R = bufs
        T.op("act", lambda e: e.activation(out=junk[0:np_, :], in_=xs_ap, func=AF.Square, accum_out=ss[0:np_, 0:1]),
             reads=[Rx], writes=[R])
        T.op("act", lambda e: e.activation(out=ss[0:np_, 2:3], in_=ss[0:np_, 0:1], func=AF.Ln, scale=1.0 / D, bias=epsc[0:np_, 0:1]),
             reads=[R, R_const], writes=[R])
        T.op("act", lambda e: e.activation(out=ss[0:np_, 3:4], in_=ss[0:np_, 2:3], func=AF.Exp, scale=-0.5), reads=[R], writes=[R])
        return ss[0:np_, 3:4], R

    _nt = [0]

    def norm_transpose(xs_ap, Rx, np_, goff, dst3, Rdst, nb, pbank):
        junk, ss, Rn, hb, Rhb = nb
        rstd, Rn_ = rms_rstd(None, xs_ap, np_, "", Rx, (junk, ss, Rn))
        _nt[0] += 1
        if _nt[0] % 2:
            T.op("act", lambda e: e.activation(out=hb[0:np_, :], in_=xs_ap, func=AF.Copy, scale=rstd),
                 reads=[Rx, Rn], writes=[Rhb])
        else:
            T.op("dve", lambda e: e.tensor_scalar(out=hb[0:np_, :], in0=xs_ap, scalar1=rstd, scalar2=None, op0=ALU.mult),
                 reads=[Rx, Rn], writes=[Rhb])
        pst = psq[pbank]
        Rp = [R_ps[2 * pbank], R_ps[2 * pbank + 1]]
        for c in range(8):
            T.op("pe", lambda e, c=c: e.matmul(pst[:, c * np_:(c + 1) * np_] if np_ < 128 else pst[:, c * 128:(c + 1) * 128],
                                               lhsT=hb[0:np_, c * 128:(c + 1) * 128], rhs=ident[0:np_, 0:np_],
                                               start=True, stop=True),
                 reads=[Rhb, R_const], writes=Rp)
        if np_ == 128:
            src = pst[:, 0:1024].rearrange("p (c n) -> p c n", c=8)
        else:
            src = pst[:, 0:8 * np_].rearrange("p (c n) -> p c n", c=8)
        gb = gcols[:, goff:goff + 8].unsqueeze(2).broadcast_to([128, 8, np_])
        T.op("dve", lambda e: e.tensor_tensor(out=dst3, in0=src, in1=gb, op=ALU.mult), reads=Rp + [R_cl], writes=[Rdst])

    def norm_pipeline(items, xbufs, R_xbufs, nbs_, pbanks):
        n = len(items)
        NBF = len(xbufs)

        def load(s_):
            it = items[s_]
            b_ = s_ % NBF
            it["xs"], it["Rx"], it["nb"], it["pb"] = xbufs[b_], R_xbufs[b_], nbs_[b_], pbanks[s_ % len(pbanks)]
            if it.get("src") is not None:
                T.dma("sp", xbufs[b_][0:it["np"], :], it["src"], R_xbufs[b_], reads=it.get("rsrc", []), writes=[R_xbufs[b_]])
            if it.get("pre") is not None:
                it["pre"](xbufs[b_], R_xbufs[b_])

        def st1(it):
            junk, ss, Rn, hb, Rhb = it["nb"]
            np_ = it["np"]
            xs_ap = it["xs"][0:np_, :]
            T.op("act", lambda e: e.activation(out=junk[0:np_, :], in_=xs_ap, func=AF.Square, accum_out=ss[0:np_, 0:1]),
                 reads=[it["Rx"]], writes=[Rn])

        def st2(it):
            junk, ss, Rn, hb, Rhb = it["nb"]
            np_ = it["np"]
            xs_ap = it["xs"][0:np_, :]
            T.op("act", lambda e: e.activation(out=ss[0:np_, 2:3], in_=ss[0:np_, 0:1], func=AF.Ln, scale=1.0 / D, bias=epsc[0:np_, 0:1]),
                 reads=[Rn, R_const], writes=[Rn])
            T.op("act", lambda e: e.activation(out=ss[0:np_, 3:4], in_=ss[0:np_, 2:3], func=AF.Exp, scale=-0.5), reads=[Rn], writes=[Rn])
            rstd = ss[0:np_, 3:4]
            _nt[0] += 1
            if _nt[0] % 3 == 0:
                T.op("act", lambda e: e.activation(out=hb[0:np_, :], in_=xs_ap, func=AF.Copy, scale=rstd),
                     reads=[it["Rx"], Rn], writes=[Rhb])
            else:
                T.op("dve", lambda e: e.tensor_scalar(out=hb[0:np_, :], in0=xs_ap, scalar1=rstd, scalar2=None, op0=ALU.mult),
                     reads=[it["Rx"], Rn], writes=[Rhb])

        def st3(it):
            junk, ss, Rn, hb, Rhb = it["nb"]
            np_ = it["np"]
            pst = psq[it["pb"]]
            Rp = [R_ps[2 * it["pb"]], R_ps[2 * it["pb"] + 1]]
            for c in range(8):
                T.op("pe", lambda e, c=c: e.matmul(pst[:, c * np_:(c + 1) * np_], lhsT=hb[0:np_, c * 128:(c + 1) * 128],
                                                   rhs=ident[0:np_, 0:np_], start=True, stop=True),
                     reads=[Rhb, R_const], writes=Rp)

        def st4(it):
            np_ = it["np"]
            pst = psq[it["pb"]]
            Rp = [R_ps[2 * it["pb"]], R_ps[2 * it["pb"] + 1]]
            src = pst[:, 0:8 * np_].rearrange("p (c n) -> p c n", c=8)
            gb = gcols[:, it["goff"]:it["goff"] + 8].unsqueeze(2).broadcast_to([128, 8, np_])
            T.op("dve", lambda e: e.tensor_tensor(out=it["dst"], in0=src, in1=gb, op=ALU.mult), reads=Rp + [R_cl], writes=[it["Rdst"]])

        for s_ in range(min(2, n)):
            load(s_)
        for s_ in range(n + 3):
            if s_ + 2 < n:
                load(s_ + 2)
            if s_ < n:
                st1(items[s_])
            if 0 <= s_ - 1 < n:
                st2(items[s_ - 1])
            if 0 <= s_ - 2 < n:
                st3(items[s_ - 2])
            if 0 <= s_ - 3 < n:
                st4(items[s_ - 3])

    def groups(a, b, g=4):
        out = []
        t = a
        while t < b:
            n = min(g, b - t)
            out.append((t, n))
            t += n
        return out

    def run_layer(seg, l):
        NT = seg["NT"]
        x_src = seg["x_in"] if l == 0 else x1_d
        R_xsrc = seg["R_x"] if l == 0 else R_x1
        kva, kvb = seg["kv"][l]
        ata, atb = seg["at"][l]
        ffa, ffb = seg["ff"][l]
        voff = seg["voff"]
        nkv = kvb - kva
        ntok = nkv * 128
        KH = 4 if l == 0 else 12
        KCH = 4 if l == 0 else 6
        win = ws_in[l]
        nff = ffb - ffa
        ntk = nff * 128
        lay = ExitStack()
        h2T = sb("h2T", [128, 8, ntk + 2], BF16, lay)
        R_h2 = [T.res("h2T%d" % i) for i in range(nff)]
        R_h2h = T.res("h2Th")
        T.op("dve", lambda e: e.memset(h2T[:, :, 0:1], 0.0), writes=[R_h2h])
        T.op("dve", lambda e: e.memset(h2T[:, :, ntk + 1:ntk + 2], 0.0), writes=[R_h2h])
        with ExitStack() as att:
            QT = sb("QT", [128, 8, ntok], BF16, att)
            KT = sb("KT", [128, KCH, ntok], BF16, att)
            VA = sb("VA", [128, nkv, KH * 65], BF16, att)
            kmT = sb("kmT", [128, 2, 256], BF16, att)
            vma = sb("vma", [128, 2, 4 * 65], BF16, att)
            R_QT, R_KT, R_VA, R_km, R_vm = T.res("QT"), T.res("KT"), T.res("VA"), T.res("kmT"), T.res("vma")
            R_VAt = [T.res("VAt%d" % i) for i in range(nkv)]
            with ExitStack() as p1:
                hT = sb("hT", [128, 8, ntok], BF16, p1)
                R_hT = [T.res("hT%d" % i) for i in range(nkv)]
                NB1 = 4
                xsb = [sb("p1x%d" % i, [128, D], F32, p1) for i in range(NB1)]
                R_xsb = [T.res("p1x%d" % i) for i in range(NB1)]
                nbs = []
                for i in range(NB1):
                    nbs.append((sb("junk%d" % i, [128, D], BF16, p1), sb("ss%d" % i, [128, 4], F32, p1), T.res("nrm%d" % i),
                                sb("hb%d" % i, [128, D], BF16, p1), T.res("hb%d" % i)))
                memT = sb("memT", [128, 8, 256], BF16, p1)
                R_memT = T.res("memT")
                wst = [sb("wst%d" % i, [128, 8, 128], BF16, p1) for i in range(3)]
                R_wst = [T.res("wst%d" % i) for i in range(3)]
                NV = 256 if l == 0 else 768
                wv = sb("wv", [128, 8, NV], BF16, p1)
                R_wv = T.res("wv")
                wmv = sb("wmv", [128, 8, 256], BF16, p1)
                R_wmv = T.res("wmv")
                evt = [0]
                R_memTt = [T.res("memT%d" % i) for i in range(2)]
                items = []
                for mt in range(2):
                    items.append(dict(src=seg["mem"][mt * 128:(mt + 1) * 128, :], np=128, goff=GMEM[l],
                                      dst=memT[:, :, mt * 128:(mt + 1) * 128], Rdst=R_memTt[mt]))
                for k in range(nkv):
                    t = kva + k
                    items.append(dict(src=x_src[t * 128:(t + 1) * 128, :], rsrc=[R_xsrc], np=128, goff=GM[l],
                                      dst=hT[:, :, k * 128:(k + 1) * 128], Rdst=R_hT[k]))
                norm_pipeline(items, xsb, R_xsb, nbs, [0, 1])
                wi = 0

                def evac(dst, src, reads, writes, scale=None):
                    evt[0] += 1
                    if evt[0] % 2:
                        if scale is None:
                            T.op("act", lambda e: e.copy(out=dst, in_=src), reads=reads, writes=writes)
                        else:
                            T.op("act", lambda e: e.activation(out=dst, in_=src, func=AF.Copy, scale=scale), reads=reads, writes=writes)
                    else:
                        if scale is None:
                            T.op("dve", lambda e: e.tensor_copy(out=dst, in_=src), reads=reads, writes=writes)
                        else:
                            T.op("dve", lambda e: e.tensor_scalar(out=dst, in0=src, scalar1=scale, scalar2=None, op0=ALU.mult),
                                 reads=reads, writes=writes)

                pb = [4]

                def nextbank():
                    pb[0] = 4 + (pb[0] - 4 + 1) % 4
                    return bank(pb[0])

                for o in range(2):
                    w = wst[wi % 3]
                    Rw = R_wst[wi % 3]
                    wi += 1
                    T.dma("sp", w[:], ws_mk[l, o], Rw, reads=[RW[("in", l)]], writes=[Rw])
                    pa, Rp = nextbank()
                    for c in range(8):
                        T.op("pe", lambda e, c=c, w=w, pa=pa: e.matmul(pa[:, 0:256], lhsT=w[:, c, :], rhs=memT[:, c, :],
                                                                       start=(c == 0), stop=(c == 7)),
                             reads=[Rw] + R_memTt, writes=[Rp])
                    evac(kmT[:, o, :], pa[:, 0:256], [Rp], [R_km])
                T.dma("sp", wmv[:], wm_mv[l].rearrange("(c p) n -> p c n", p=128), R_wmv, reads=[RW[("in", l)]], writes=[R_wmv])
                T.op("dve", lambda e: e.memset(vma[:], 1.0), writes=[R_vm])
                for mt in range(2):
                    pa, Rp = nextbank()
                    for c in range(8):
                        T.op("pe", lambda e, c=c, pa=pa, mt=mt: e.matmul(pa[:, 0:256], lhsT=memT[:, c, mt * 128:(mt + 1) * 128],
                                                                         rhs=wmv[:, c, :], start=(c == 0), stop=(c == 7)),
                             reads=[R_wmv, R_memTt[mt]], writes=[Rp])
                    dst = vma[:, mt, :].rearrange("p (h d) -> p h d", h=4)[:, :, 0:64]
                    src = pa[:, 0:256].rearrange("p (h d) -> p h d", h=4)
                    evac(dst, src, [Rp], [R_vm])
                if l == 0:
                    chunks = [("q", o, o) for o in range(6)] + [("k", g, 6 + g) for g in range(4)] + [("q", 6 + o, 10 + o) for o in range(2)]
                else:
                    chunks = [("q", o, o) for o in range(6)] + [("k", o, 6 + o) for o in range(6)] + [("q", 6 + o, 12 + o) for o in range(2)]
                for kind, dchunk, wchunk in chunks:
                    w = wst[wi % 3]
                    Rw = R_wst[wi % 3]
                    wi += 1
                    T.dma("sp", w[:], win[wchunk], Rw, reads=[RW[("in", l)]], writes=[Rw])
                    for (g0, n) in groups(0, nkv):
                        pa, Rp = nextbank()
                        for c in range(8):
                            T.op("pe", lambda e, c=c, w=w, pa=pa, g0=g0, n=n: e.matmul(
                                pa[:, 0:n * 128], lhsT=w[:, c, :], rhs=hT[:, c, g0 * 128:(g0 + n) * 128],
                                start=(c == 0), stop=(c == 7)), reads=[Rw] + R_hT[g0:g0 + n], writes=[Rp])
                        if kind == "q":
                            evac(QT[:, dchunk, g0 * 128:(g0 + n) * 128], pa[:, 0:n * 128], [Rp], [R_QT], scale=0.125)
                        else:
                            evac(KT[:, dchunk, g0 * 128:(g0 + n) * 128], pa[:, 0:n * 128], [Rp], [R_KT])
                T.dma("sp", wv[:], wm_v[l].rearrange("(c p) n -> p c n", p=128), R_wv, reads=[RW[("in", l)]], writes=[R_wv])
                npiece = 1 if l == 0 else 2
                PW = NV // npiece
                HPP = PW // 64
                for k in range(nkv):
                    t = kva + k
                    vcol = vfl[:, voff + t:voff + t + 1]
                    for pc in range(npiece):
                        pa, Rp = nextbank()
                        for c in range(8):
                            T.op("pe", lambda e, c=c, pa=pa, k=k, pc=pc: e.matmul(
                                pa[:, 0:PW], lhsT=hT[:, c, k * 128:(k + 1) * 128], rhs=wv[:, c, pc * PW:(pc + 1) * PW],
                                start=(c == 0), stop=(c == 7)), reads=[R_wv, R_hT[k]], writes=[Rp])
                        dst = VA[:, k, pc * HPP * 65:(pc + 1) * HPP * 65].rearrange("p (h d) -> p h d", h=HPP)[:, :, 0:64]
                        src = pa[:, 0:PW].rearrange("p (h d) -> p h d", h=HPP)
                        evt[0] += 1
                        if evt[0] % 2:
                            T.op("act", lambda e, dst=dst, src=src, vcol=vcol: e.activation(
                                out=dst, in_=src, func=AF.Copy, scale=vcol), reads=[Rp, R_cl], writes=[R_VAt[k]])
                        else:
                            T.op("dve", lambda e, dst=dst, src=src, vcol=vcol: e.tensor_scalar(
                                out=dst, in0=src, scalar1=vcol, scalar2=None, op0=ALU.mult), reads=[Rp, R_cl], writes=[R_VAt[k]])
                    onesd = VA[:, k, :].rearrange("p (h d) -> p h d", h=KH)[:, :, 64:65]
                    T.op("dve", lambda e, onesd=onesd, vcol=vcol: e.tensor_copy(
                        out=onesd, in_=vcol.unsqueeze(1).broadcast_to([128, KH, 1])), reads=[R_cl], writes=[R_VAt[k]])
            T.barrier()
            with ExitStack() as p2:
                xsb = [sb("p2x%d" % i, [128, D], F32, p2) for i in range(3)]
                R_xsb = [T.res("p2x%d" % i) for i in range(3)]
                R_bt = [T.res("btab%d" % h) for h in range(12)]
                if l == 0:
                    btab = sb("alibi_hi", [128, 12 * 3 * 128], BF16, p2)
                    btab2 = sb("alibi_lo", [128, 12 * 3 * 128], BF16, p2)
                    for h in range(0, 12, 3):
                        T.dma("pool", btab[:, h * 384:(h + 3) * 384], alibi_d[:, h * 384:(h + 3) * 384], R_bt[h], writes=R_bt[h:h + 3])
                        T.dma("pool", btab2[:, h * 384:(h + 3) * 384], alibi2_d[:, h * 384:(h + 3) * 384], R_bt[h], writes=R_bt[h:h + 3])
                else:
                    btab = sb("naint_sb", [128, 12 * 5 * 128], BF16, p2)
                    for h in range(0, 12, 3):
                        T.dma("pool", btab[:, h * 640:(h + 3) * 640], naint_d[:, h * 640:(h + 3) * 640], R_bt[h], writes=R_bt[h:h + 3])
                    edg = [sb("edg%d" % i, [128, 768], BF16, p2) for i in range(4)]
                    R_edg = [T.res("edg%d" % i) for i in range(4)]
                wo = sb("wo", [128, 8, D], BF16, p2)
                R_wo = T.res("wo")
                T.dma("sp", wo[:], wm_o[l].rearrange("(c p) n -> p c n", p=128), R_wo, reads=[RW[("o", l)]], writes=[R_wo])
                NPB = 6
                PT = [sb("PT%d" % i, [128, 768], BF16, p2) for i in range(NPB)]
                R_PT = [T.res("PT%d" % i) for i in range(NPB)]
                osb = [sb("o%d" % i, [128, D], BF16, p2) for i in range(2)]
                R_o = [T.res("o%d" % i) for i in range(2)]
                oT = [sb("oT%d" % i, [128, 8, 128], BF16, p2) for i in range(2)]
                R_oT = [T.res("oT%d" % i) for i in range(2)]
                R_oTh = [[T.res("oTh%d_%d" % (i, j)) for j in range(2)] for i in range(2)]
                den = [sb("den%d" % i, [128, 32], F32, p2) for i in range(2)]
                R_den = [T.res("den%d" % i) for i in range(2)]
                LAG = 3 if l == 0 else 1
                n2junk = sb("n2junk", [128, D], BF16, p2)
                n2ss = [sb("n2ss%d" % i_, [128, 4], F32, p2) for i_ in range(2)]
                n2hb = [sb("n2hb%d" % i_, [128, D], BF16, p2) for i_ in range(2)]
                R_n2 = [T.res("n2_%d" % i_) for i_ in range(2)]
                R_n2hb = [T.res("n2hb%d" % i_) for i_ in range(2)]
                sbk = [0]
                ps4bf = ps_all[:, 2048:2560].bitcast(BF16)
                pend_epi = [None, None, None]
                hc = [0]
                ec = [0]
                tiles = list(range(ata, atb))
                if tiles:
                    t0 = tiles[0]
                    T.dma("sp", xsb[0][:], x_src[t0 * 128:(t0 + 1) * 128, :], R_xsb[0], reads=[R_xsrc], writes=[R_xsb[0]])
                for ti, t in enumerate(tiles):
                    i = ti % 2
                    xi = ti % 3
                    if ti + 1 < len(tiles):
                        tn = tiles[ti + 1]
                        xn = (ti + 1) % 3
                        T.dma("sp", xsb[xn][:], x_src[tn * 128:(tn + 1) * 128, :], R_xsb[xn], reads=[R_xsrc],
                              writes=[R_xsb[xn]])
                    kq = t - kva
                    pvA, R_pvA = bank(5)
                    pvB, R_pvB = bank(6)
                    pvM, R_pvM = bank(7)
                    tasks = []
                    for h in range(12):
                        tk = dict(kind="mix", h=h, edge=None)
                        if l == 0:
                            js = [j for j in (t - 1, t, t + 1) if kva <= j < kvb]
                            jrel0 = js[0] - (t - 1)
                            nk = len(js)
                            tk["bsrc"] = btab[:, (h * 3 + jrel0) * 128:(h * 3 + jrel0 + nk) * 128]
                            tk["bsrc2"] = btab2[:, (h * 3 + jrel0) * 128:(h * 3 + jrel0 + nk) * 128]
                            tk["Rb"] = R_bt[h]
                            g = h // 3
                            tk["kch"], tk["kb"], tk["vh"] = g, 64 * (h % 2), g
                        else:
                            slots = EDGE_SLOTS_P if seg["kind"] == "p" else EDGE_SLOTS_S
                            wins = EDGE_WIN_P if seg["kind"] == "p" else EDGE_WIN_S
                            if t in slots:
                                ja, jb = wins[t]
                                js = list(range(ja, jb))
                                nk = len(js)
                                si = slots.index(t)
                                if seg["kind"] == "p":
                                    tk["edge"] = edgep_d[si][:, h * 4 * 128:(h * 4 + nk) * 128]
                                else:
                                    tk["edge"] = edges_d[seg["sub"], si][:, h * 6 * 128:(h * 6 + nk) * 128]
                            else:
                                js = list(range(t - 2, t + 3))
                                nk = 5
                                tk["bsrc"] = btab[:, h * 640:(h + 1) * 640]
                                tk["bsrc2"] = None
                                tk["Rb"] = R_bt[h]
                            tk["kch"], tk["kb"], tk["vh"] = h // 2, 64 * (h % 2), h
                        for j in js:
                            assert kva <= j < kvb, (seg["kind"], l, t, j)
                        tk["js"], tk["nk"] = js, nk
                        tk["qch"], tk["qb"] = h // 2, 64 * (h % 2)
                        tk["pv"], tk["Rpv"] = (pvA, R_pvA) if h < 6 else (pvB, R_pvB)
                        tk["hh"] = h % 6
                        tasks.append(tk)
                    mixt = tasks
                    tasks = []
                    for h in range(12):
                        tasks.append(mixt[h])
                        if h in (1, 4):
                            tasks.append(dict(kind="mem", m=h // 3, nk=2))
                    tasks.append(dict(kind="mem", m=2, nk=2))
                    tasks.append(dict(kind="mem", m=3, nk=2))

                    def emitA(tk, k):
                        pi = k % NPB
                        tk["pi"] = pi
                        nk = tk["nk"]
                        subs = []
                        for i0 in range(0, nk, 4):
                            bk_, Rb_ = bank(sbk[0] % 4)
                            sbk[0] += 1
                            subs.append((bk_, Rb_, i0, min(4, nk - i0)))
                        if tk["kind"] == "mix":
                            if tk["edge"] is not None:
                                ei = ec[0] % 4
                                ec[0] += 1
                                T.dma("pool", edg[ei][:, 0:nk * 128], tk["edge"], R_edg[ei], writes=[R_edg[ei]])
                                tk["bsrc"] = edg[ei][:, 0:nk * 128]
                                tk["bsrc2"] = None
                                tk["Rb"] = R_edg[ei]
                            kch, kb, qch, qb = tk["kch"], tk["kb"], tk["qch"], tk["qb"]
                            bsrc, bsrc2 = tk["bsrc"], tk["bsrc2"]
                            for (S, RS, i0, cnt) in subs:
                                T.op("pe", lambda e, S=S, i0=i0, cnt=cnt, bsrc=bsrc: e.matmul(
                                    S[:, 0:cnt * 128], lhsT=ident[:], rhs=bsrc[:, i0 * 128:(i0 + cnt) * 128], start=True, stop=False),
                                    reads=[tk["Rb"], R_const], writes=[RS])
                                if bsrc2 is not None:
                                    T.op("pe", lambda e, S=S, i0=i0, cnt=cnt, bsrc2=bsrc2: e.matmul(
                                        S[:, 0:cnt * 128], lhsT=ident[:], rhs=bsrc2[:, i0 * 128:(i0 + cnt) * 128], start=False, stop=False),
                                        reads=[tk["Rb"], R_const], writes=[RS])
                                for idx in range(i0, i0 + cnt):
                                    kj = tk["js"][idx] - kva
                                    T.op("pe", lambda e, S=S, idx=idx, i0=i0, cnt=cnt, kj=kj, kch=kch, kb=kb, qch=qch, qb=qb: e.matmul(
                                        S[:, (idx - i0) * 128:(idx - i0 + 1) * 128], lhsT=KT[kb:kb + 64, kch, kj * 128:(kj + 1) * 128],
                                        rhs=QT[qb:qb + 64, qch, kq * 128:(kq + 1) * 128], start=False, stop=(idx == i0 + cnt - 1)),
                                        reads=[R_KT, R_QT], writes=[RS])
                            for (S, RS, i0, cnt) in subs:
                                T.op("act", lambda e, S=S, i0=i0, cnt=cnt, pi=pi: e.activation(
                                    out=PT[pi][:, i0 * 128:(i0 + cnt) * 128], in_=S[:, 0:cnt * 128], func=AF.Exp),
                                    reads=[RS], writes=[R_PT[pi]])
                        else:
                            m = tk["m"]
                            mb = 64 * (m % 2)
                            S, RS = subs[0][0], subs[0][1]
                            for mt in range(2):
                                T.op("pe", lambda e, S=S, mt=mt, m=m, mb=mb: e.matmul(
                                    S[:, mt * 128:(mt + 1) * 128], lhsT=kmT[mb:mb + 64, m // 2, mt * 128:(mt + 1) * 128],
                                    rhs=QT[mb:mb + 64, 6 + m // 2, kq * 128:(kq + 1) * 128], start=True, stop=True),
                                    reads=[R_km, R_QT], writes=[RS])
                            T.op("act", lambda e, S=S, pi=pi: e.activation(out=PT[pi][:, 0:256], in_=S[:, 0:256], func=AF.Exp),
                                 reads=[RS], writes=[R_PT[pi]])

                    def emitB(tk):
                        pi = tk["pi"]
                        nk = tk["nk"]
                        if tk["kind"] == "mix":
                            pv, Rpv, hh, vh = tk["pv"], tk["Rpv"], tk["hh"], tk["vh"]
                            for idx, j in enumerate(tk["js"]):
                                kj = j - kva
                                T.op("pe", lambda e, pv=pv, hh=hh, idx=idx, kj=kj, vh=vh, pi=pi, nk=nk: e.matmul(
                                    pv[:, hh * 65:(hh + 1) * 65], lhsT=PT[pi][:, idx * 128:(idx + 1) * 128],
                                    rhs=VA[:, kj, vh * 65:(vh + 1) * 65], start=(idx == 0), stop=(idx == nk - 1)),
                                    reads=[R_PT[pi], R_VAt[kj]], writes=[Rpv])
                        else:
                            m = tk["m"]
                            for mt in range(2):
                                T.op("pe", lambda e, mt=mt, m=m, pi=pi: e.matmul(
                                    pvM[:, m * 65:(m + 1) * 65], lhsT=PT[pi][:, mt * 128:(mt + 1) * 128],
                                    rhs=vma[:, mt, m * 65:(m + 1) * 65], start=(mt == 0), stop=(mt == 1)),
                                    reads=[R_PT[pi], R_vm], writes=[R_pvM])

                    dn = den[i]
                    Rd = R_den[i]
                    ob = osb[i]
                    vcol = vfl[:, voff + t:voff + t + 1]

                    def emit_norm(pv, Rpv, c0, nh):
                        dsrc = pv[:, 0:nh * 65].rearrange("p (h d) -> p h d", h=nh)[:, :, 64:65]
                        ddst = dn[:, c0:c0 + nh].unsqueeze(2)
                        if l == 0 and c0 < 12:
                            T.op("dve", lambda e: e.tensor_tensor(
                                out=ddst, in0=dsrc, in1=expsink[:, c0:c0 + nh].unsqueeze(2), op=ALU.add),
                                reads=[Rpv, R_cl], writes=[Rd])
                        else:
                            T.op("dve", lambda e: e.tensor_scalar(
                                out=ddst, in0=dsrc, scalar1=1e-30, scalar2=None, op0=ALU.max), reads=[Rpv], writes=[Rd])
                        T.op("dve", lambda e: e.reciprocal(out=dn[:, 16 + c0:16 + c0 + nh], in_=dn[:, c0:c0 + nh]), reads=[Rd], writes=[Rd])
                        src = pv[:, 0:nh * 65].rearrange("p (h d) -> p h d", h=nh)[:, :, 0:64]
                        dst = ob[:, c0 * 64:(c0 + nh) * 64].rearrange("p (h d) -> p h d", h=nh)
                        rb = dn[:, 16 + c0:16 + c0 + nh].unsqueeze(2).broadcast_to([128, nh, 64])
                        T.op("dve", lambda e: e.scalar_tensor_tensor(out=dst, in0=src, scalar=vcol, in1=rb, op0=ALU.mult, op1=ALU.mult),
                             reads=[Rpv, Rd, R_cl], writes=[R_o[i]])

                    pending = {"A": 6, "B": 6, "M": 4}

                    def doB(tk):
                        emitB(tk)
                        g = "M" if tk["kind"] == "mem" else ("A" if tk["h"] < 6 else "B")
                        pending[g] -= 1
                        if pending[g] == 0:
                            if g == "A":
                                emit_norm(pvA, R_pvA, 0, 6)
                            elif g == "B":
                                emit_norm(pvB, R_pvB, 6, 6)
                            else:
                                emit_norm(pvM, R_pvM, 12, 4)

                    def make_epi(t=t, i=i, xi=xi, ob=ob):
                        def e1(part):
                            pa, Rp = bank(4)
                            for c in range(4 * part, 4 * part + 4):
                                T.op("pe", lambda e, c=c: e.matmul(pa[:, (c % 4) * 128:(c % 4 + 1) * 128], lhsT=ob[:, c * 128:(c + 1) * 128],
                                                                   rhs=ident[:], start=True, stop=True),
                                     reads=[R_o[i], R_const], writes=[Rp])
                            if part == 0:
                                T.op("act", lambda e: e.copy(out=oT[i][:, 0:4, :], in_=pa[:, 0:512].rearrange("p (c n) -> p c n", c=4)),
                                     reads=[Rp], writes=[R_oTh[i][0]])
                            else:
                                T.op("dve", lambda e: e.tensor_copy(out=oT[i][:, 4:8, :], in_=pa[:, 0:512].rearrange("p (c n) -> p c n", c=4)),
                                     reads=[Rp], writes=[R_oTh[i][1]])

                        def ehalf(half):
                            pa, Rp = bank(4)
                            for c in range(8):
                                T.op("pe", lambda e, c=c: e.matmul(
                                    pa[:, 0:512], lhsT=oT[i][:, c, :], rhs=wo[:, c, half * 512:(half + 1) * 512],
                                    start=(c == 0), stop=(c == 7)), reads=[R_oTh[i][c // 4], R_wo], writes=[Rp])
                            T.op("dve", lambda e: e.tensor_tensor(
                                out=xsb[xi][:, half * 512:(half + 1) * 512], in0=xsb[xi][:, half * 512:(half + 1) * 512],
                                in1=pa[:, 0:512], op=ALU.add), reads=[Rp, R_xsb[xi]], writes=[R_xsb[xi]])

                        def e3():
                            ehalf(1)
                            T.dma("sp", xm_d[t * 128:(t + 1) * 128, :], xsb[xi][:], R_xsb[xi], reads=[R_xsb[xi]], writes=[R_xm])
                        need = (ffa - 1 <= t <= ffb)

                        def e_norm():
                            if not need:
                                return
                            ss, hb, Rn, Rhb = n2ss[i], n2hb[i], R_n2[i], R_n2hb[i]
                            xs_ap = xsb[xi][:]
                            T.op("act", lambda e: e.activation(out=n2junk[:], in_=xs_ap, func=AF.Square, accum_out=ss[:, 0:1]),
                                 reads=[R_xsb[xi]], writes=[Rn])
                            T.op("act", lambda e: e.activation(out=ss[:, 2:3], in_=ss[:, 0:1], func=AF.Ln, scale=1.0 / D, bias=epsc[:, 0:1]),
                                 reads=[Rn, R_const], writes=[Rn])
                            T.op("act", lambda e: e.activation(out=ss[:, 3:4], in_=ss[:, 2:3], func=AF.Exp, scale=-0.5), reads=[Rn], writes=[Rn])
                            if i == 0:
                                T.op("act", lambda e: e.activation(out=hb[:], in_=xs_ap, func=AF.Copy, scale=ss[:, 3:4]),
                                     reads=[R_xsb[xi], Rn], writes=[Rhb])
                            else:
                                T.op("dve", lambda e: e.tensor_scalar(out=hb[:], in0=xs_ap, scalar1=ss[:, 3:4], scalar2=None, op0=ALU.mult),
                                     reads=[R_xsb[xi], Rn], writes=[Rhb])

                        def e_t(part):
                            if not need:
                                return
                            hb, Rhb = n2hb[i], R_n2hb[i]
                            pa, Rp = bank(4)
                            for c in range(4 * part, 4 * part + 4):
                                T.op("pe", lambda e, c=c: e.matmul(pa[:, (c % 4) * 128:(c % 4 + 1) * 128], lhsT=hb[:, c * 128:(c + 1) * 128],
                                                                   rhs=ident[:], start=True, stop=True),
                                     reads=[Rhb, R_const], writes=[Rp])
                            src = pa[:, 0:512].rearrange("p (c n) -> p c n", c=4)
                            goff = GFFN[l] + 4 * part
                            if ffa <= t < ffb:
                                k_ = t - ffa
                                dst = h2T[:, 4 * part:4 * part + 4, 1 + k_ * 128:1 + (k_ + 1) * 128]
                                gb = gcols[:, goff:goff + 4].unsqueeze(2).broadcast_to([128, 4, 128])
                                T.op("dve", lambda e: e.tensor_tensor(out=dst, in0=src, in1=gb, op=ALU.mult),
                                     reads=[Rp, R_cl], writes=[R_h2[k_]])
                            else:
                                col = 127 if t == ffa - 1 else 0
                                dcol = 0 if t == ffa - 1 else ntk + 1
                                dst = h2T[:, 4 * part:4 * part + 4, dcol:dcol + 1]
                                gb = gcols[:, goff:goff + 4].unsqueeze(2)
                                T.op("dve", lambda e: e.tensor_tensor(out=dst, in0=src[:, :, col:col + 1], in1=gb, op=ALU.mult),
                                     reads=[Rp, R_cl], writes=[R_h2h])

                        def e3n():
                            e3()
                            e_norm()
                        return [lambda: e1(0), lambda: e1(1), lambda: ehalf(0), e3n], [lambda: e_t(0), lambda: e_t(1)]

                    prev = pend_epi[0]
                    prevt = pend_epi[1]
                    for k, tk in enumerate(tasks):
                        emitA(tk, hc[0])
                        hc[0] += 1
                        if prev is not None:
                            if k == 1:
                                prev[0]()
                            elif k == 3:
                                prev[1]()
                            elif k == 5:
                                prev[2]()
                            elif k == 8:
                                prev[3]()
                        if prevt is not None:
                            if k == 11:
                                prevt[0]()
                            elif k == 13:
                                prevt[1]()
                        if k >= LAG:
                            doB(tasks[k - LAG])
                    for k in range(max(0, len(tasks) - LAG), len(tasks)):
                        doB(tasks[k])
                    pend_epi[1] = pend_epi[2]
                    pend_epi[0], pend_epi[2] = make_epi()
                if pend_epi[1] is not None:
                    for f_ in pend_epi[1]:
                        f_()
                if pend_epi[0] is not None:
                    for f_ in pend_epi[0]:
                        f_()
                    for f_ in pend_epi[2]:
                        f_()
            T.barrier()
        with ExitStack() as p3:
            aT = sb("aT", [128, NFC, ntk], BF16, p3)
            R_aT = T.res("aT")
            p3b = p3
            wd = sb("wd", [128, NFC, D], BF16, p3b)
            R_wdc = [T.res("wd%d" % c) for c in range(NFC // 2)]

            def load_wd(cp):
                c0 = 2 * cp
                T.dma("sp", wd[:, c0:c0 + 2, :], wm_d[l][c0 * 128:(c0 + 2) * 128, :].rearrange("(c p) n -> p c n", p=128),
                      R_wdc[cp], reads=[RW[("d", l)]], writes=[R_wdc[cp]])
            xsb = [sb("p4x%d" % i, [128, D], F32, p3b) for i in range(3)]
            R_xsb = [T.res("p4x%d" % i) for i in range(3)]
            if l == 1:
                gfin = sb("gfin_sb", [128, D], F32, p3b)
                R_gf = T.res("gfin")
                T.dma("sp", gfin[:], gfin_d[:, :], R_gf, writes=[R_gf])
                ysb1 = sb("y0", [128, D], F32, p3b)
                ysb = [ysb1, ysb1]
                R_y1 = T.res("y0")
                R_y = [R_y1, R_y1]
                junk = sb("junkf", [128, D], BF16, p3b)
                ssf = [sb("ssf%d" % i, [128, 4], F32, p3b) for i in range(2)]
                R_nf = [T.res("nf%d" % i) for i in range(2)]
            NWB = 3 if l == 0 else 2
            with ExitStack() as p3a:
                wg = [sb("wg%d" % i, [128, 8, 128], BF16, p3a) for i in range(NWB)]
                wu = [sb("wu%d" % i, [128, 8, 128], BF16, p3a) for i in range(NWB)]
                R_wg = [T.res("wg%d" % i) for i in range(NWB)]
                R_wu = [T.res("wu%d" % i) for i in range(NWB)]
                gbuf = [sb("gbuf%d" % i, [128, 514], F32, p3a) for i in range(2)]
                R_gb = [T.res("gbuf%d" % i) for i in range(2)]
                cv = [sb("cv%d" % i, [128, 512], F32, p3a) for i in range(2)]
                R_cv = [T.res("cv%d" % i) for i in range(2)]
                sg = cv
                R_sg = R_cv
                it = 0
                grp = groups(0, nff)

                def loadw(c):
                    T.dma("sp", wg[c % NWB][:], ws_g[l, c], R_wg[c % NWB], reads=[RW[("gu", l)]], writes=[R_wg[c % NWB]])
                    T.dma("sp", wu[c % NWB][:], ws_u[l, c], R_wu[c % NWB], reads=[RW[("gu", l)]], writes=[R_wu[c % NWB]])

                for c_ in range(NWB - 1):
                    loadw(c_)
                for c in range(NFC):
                    if c + NWB - 1 < NFC:
                        loadw(c + NWB - 1)
                    if c < NFC // 2:
                        load_wd(c)
                    cw = convp[:, (l * NFC + c) * 4:(l * NFC + c) * 4 + 4]
                    for (g0, n) in grp:
                        b = it % 2
                        it += 1
                        n128 = n * 128
                        c0 = g0 * 128
                        pg, Rpg = bank(0 + b)
                        pt, Rpt = bank(2)
                        pu, Rpu = bank(3 + b)
                        hres = R_h2[g0:g0 + n] + [R_h2h] + ([R_h2[g0 - 1]] if g0 > 0 else []) + ([R_h2[g0 + n]] if g0 + n < nff else [])
                        for ci in range(8):
                            T.op("pe", lambda e, ci=ci, pg=pg, c=c, c0=c0, n128=n128: e.matmul(
                                pg[:, 0:n128], lhsT=wg[c % NWB][:, ci, :], rhs=h2T[:, ci, c0:c0 + n128],
                                start=(ci == 0), stop=(ci == 7)), reads=[R_wg[c % NWB]] + hres, writes=[Rpg])
                        for ci in range(8):
                            T.op("pe", lambda e, ci=ci, pt=pt, b=b, c=c, c0=c0, n128=n128: e.matmul(
                                pt[:, 2 * b:2 * b + 2], lhsT=wg[c % NWB][:, ci, :], rhs=h2T[:, ci, c0 + n128:c0 + n128 + 2],
                                start=(ci == 0), stop=(ci == 7)), reads=[R_wg[c % NWB]] + hres, writes=[Rpt])
                        for ci in range(8):
                            T.op("pe", lambda e, ci=ci, pu=pu, c=c, c0=c0, n128=n128: e.matmul(
                                pu[:, 0:n128], lhsT=wu[c % NWB][:, ci, :], rhs=h2T[:, ci, c0 + 1:c0 + 1 + n128],
                                start=(ci == 0), stop=(ci == 7)), reads=[R_wu[c % NWB]] + hres, writes=[Rpu])
                        gb_ = gbuf[b]
                        T.op("act", lambda e, gb_=gb_, pg=pg, n128=n128: e.copy(out=gb_[:, 0:n128], in_=pg[:, 0:n128]),
                             reads=[Rpg], writes=[R_gb[b]])
                        T.op("act", lambda e, gb_=gb_, pt=pt, b=b, n128=n128: e.copy(out=gb_[:, n128:n128 + 2], in_=pt[:, 2 * b:2 * b + 2]),
                             reads=[Rpt], writes=[R_gb[b]])
                        cvb = cv[b]
                        T.op("dve", lambda e, cvb=cvb, gb_=gb_, cw=cw, n128=n128: e.tensor_scalar(
                            out=cvb[:, 0:n128], in0=gb_[:, 0:n128], scalar1=cw[:, 0:1], scalar2=None, op0=ALU.mult),
                            reads=[R_gb[b], R_cl], writes=[R_cv[b]])
                        T.op("dve", lambda e, cvb=cvb, gb_=gb_, cw=cw, n128=n128: e.scalar_tensor_tensor(
                            out=cvb[:, 0:n128], in0=gb_[:, 1:1 + n128], scalar=cw[:, 1:2], in1=cvb[:, 0:n128],
                            op0=ALU.mult, op1=ALU.add), reads=[R_gb[b], R_cl, R_cv[b]], writes=[R_cv[b]])
                        T.op("dve", lambda e, cvb=cvb, gb_=gb_, cw=cw, n128=n128: e.scalar_tensor_tensor(
                            out=cvb[:, 0:n128], in0=gb_[:, 2:2 + n128], scalar=cw[:, 2:3], in1=cvb[:, 0:n128],
                            op0=ALU.mult, op1=ALU.add), reads=[R_gb[b], R_cl, R_cv[b]], writes=[R_cv[b]])
                        sgb = sg[b]
                        T.op("act", lambda e, sgb=sgb, cvb=cvb, cw=cw, n128=n128: e.activation(
                            out=sgb[:, 0:n128], in_=cvb[:, 0:n128], func=AF.Silu, bias=cw[:, 3:4]),
                            reads=[R_cv[b], R_cl], writes=[R_sg[b]])
                        T.op("dve", lambda e, sgb=sgb, pu=pu, c=c, g0=g0, n128=n128: e.tensor_tensor(
                            out=aT[:, c, g0 * 128:g0 * 128 + n128], in0=sgb[:, 0:n128], in1=pu[:, 0:n128], op=ALU.mult),
                            reads=[R_sg[b], Rpu], writes=[R_aT])
            if True:
                def p4load(k):
                    if k < nff:
                        T.dma("sp", xsb[k % 3][:], xm_d[(ffa + k) * 128:(ffa + k + 1) * 128, :], R_xsb[k % 3], reads=[R_xm],
                              writes=[R_xsb[k % 3]])

                p4load(0)
                p4load(1)
                NG0 = min(4, nff)
                for c in range(NFC):
                    for k in range(NG0):
                        for half in range(2):
                            pa, Rp = bank((k % 4) * 2 + half)
                            T.op("pe", lambda e, c=c, pa=pa, half=half, k=k: e.matmul(
                                pa[:, 0:512], lhsT=aT[:, c, k * 128:(k + 1) * 128], rhs=wd[:, c, half * 512:(half + 1) * 512],
                                start=(c == 0), stop=(c == NFC - 1)), reads=[R_aT, R_wdc[c // 2]], writes=[Rp])
                for k in range(nff):
                    t = ffa + k
                    i = k % 3
                    p4load(k + 2)
                    for half in range(2):
                        pa, Rp = bank((k % 4) * 2 + half)
                        for c in range(NFC if k >= NG0 else 0):
                            T.op("pe", lambda e, c=c, pa=pa, half=half, k=k: e.matmul(
                                pa[:, 0:512], lhsT=aT[:, c, k * 128:(k + 1) * 128], rhs=wd[:, c, half * 512:(half + 1) * 512],
                                start=(c == 0), stop=(c == NFC - 1)), reads=[R_aT, R_wdc[c // 2]], writes=[Rp])
                        T.op("dve", lambda e, pa=pa, half=half, i=i: e.tensor_tensor(
                            out=xsb[i][:, half * 512:(half + 1) * 512], in0=xsb[i][:, half * 512:(half + 1) * 512],
                            in1=pa[:, 0:512], op=ALU.add), reads=[Rp, R_xsb[i]], writes=[R_xsb[i]])
                    if l == 0:
                        T.dma("sp", x1_d[t * 128:(t + 1) * 128, :], xsb[i][:], R_xsb[i], reads=[R_xsb[i]], writes=[R_x1])
                    else:
                        j = k % 2
                        rstd, Rn = rms_rstd(None, xsb[i][:], 128, "", R_xsb[i], (junk, ssf[j], R_nf[j]))
                        T.op("dve", lambda e, i=i, j=j, rstd=rstd: e.scalar_tensor_tensor(
                            out=ysb[j][:], in0=xsb[i][:], scalar=rstd, in1=gfin[:], op0=ALU.mult, op1=ALU.mult),
                            reads=[R_xsb[i], Rn, R_gf], writes=[R_y[j]])
                        oa, ob_ = seg["own"]
                        if oa <= t < ob_:
                            T.dma("sp", seg["y_out"][(t - oa) * 128:(t - oa + 1) * 128, :], ysb[j][:], R_y[j],
                                  reads=[R_y[j]], writes=[seg["R_y"]])
            T.barrier()
        lay.close()

    segs = []
    for s in range(4):
        segs.append(dict(kind="p", NT=NT_P, x_in=xp[s], R_x=T.res("xin"), mem=memp[s], voff=0,
                         kv=[(0, 16), (0, 16)], at=[(0, 16), (0, 16)], ff=[(0, 16), (0, 16)], own=(0, 16),
                         y_out=yp[s], R_y=T.res("yout")))
    for j in range(2):
        segs.append(dict(kind="s", sub=j, NT=NT_S, x_in=xs_in[j], R_x=T.res("xin"), mem=mems, voff=NT_S * (1 + j),
                         kv=[(0, 18), (1, 17)], at=[(1, 17), (3, 15)], ff=[(1, 17), (5, 13)], own=(5, 13),
                         y_out=ys[j], R_y=T.res("yout")))
    import os
    nseg = int(os.environ.get("MK_NSEG", "6"))
    for seg in segs[:nseg] if nseg > 0 else segs[4:4 - nseg]:
        for l in range(2):
            run_layer(seg, l)
    T.barrier()
    st.close()
    return nc, T.ninst


_PROG = None


def kernel(x_prompt, x_sample, mem_prompt, mem_sample, g_mix, g_mem, w_in_a, sink_a, w_in_b, rpb_b,
           w_mem_kv, w_o, g_ffn, w_gate, w_up, conv_w, conv_b, w_down, g_final):
    global _PROG
    f = lambda a: np.ascontiguousarray(np.asarray(a, dtype=np.float32))
    x_prompt, x_sample, mem_prompt, mem_sample = f(x_prompt), f(x_sample), f(mem_prompt), f(mem_sample)
    g_mix, g_mem, g_ffn, g_final = f(g_mix), f(g_mem), f(g_ffn), f(g_final)
    conv_w, conv_b, sink_a, rpb_b = f(conv_w), f(conv_b), f(sink_a), f(rpb_b)
    if _PROG is None:
        _PROG = build_program()
    nc, _ = _PROG

    def gcol(v):
        return v.reshape(8, 128).T
    gcols = np.concatenate([gcol(g_mix[0]), gcol(g_mix[1]), gcol(g_mem[0]), gcol(g_mem[1]),
                            gcol(g_ffn[0]), gcol(g_ffn[1]), gcol(g_final)], axis=1)
    gcols = np.ascontiguousarray(gcols, dtype=np.float32)
    gfin = np.ascontiguousarray(np.broadcast_to(g_final[None, :], (128, D)))
    cp = np.concatenate([conv_w, conv_b[:, None, :]], axis=1)
    cp = cp.reshape(2, 4, NFC, 128).transpose(3, 0, 2, 1)
    convp = np.ascontiguousarray(cp.reshape(128, 2 * NFC * 4))
    sinkb = np.ascontiguousarray(np.broadcast_to(sink_a[0][None, :], (128, 12)))
    import ml_dtypes
    alibi_full = _alibi_table().reshape(128, -1)
    alibi = alibi_full.astype(ml_dtypes.bfloat16).astype(np.float32)
    alibi2 = (alibi_full - alibi).astype(np.float32)
    rpb_pad = np.concatenate([rpb_b[0].reshape(12, 15 * 31), np.full((12, 1), NEG, np.float32)], axis=1)
    naint = _gather_na(rpb_pad, _na_index(100, [98, 99, 100, 101, 102], 1000)).reshape(128, -1)
    edgep = np.zeros((4, 128, 12, 4, 128), np.float32)
    for si, t in enumerate(EDGE_SLOTS_P):
        ja, jb = EDGE_WIN_P[t]
        edgep[si] = _gather_na(rpb_pad, _na_index(t, list(range(ja, jb)), 32))
    edgep = edgep.reshape(4, 128, -1)
    SR = 256
    in_maps = []
    for c in range(NCORES):
        xs = np.zeros((2, NT_S * 128, D), np.float32)
        vf = np.ones((128, 3 * NT_S), np.float32)
        edges = np.full((2, 4, 128, 12, 6, 128), NEG, np.float32)
        for j in range(2):
            g0 = 16 * c + 8 * j - S_OFF
            for lt in range(NT_S):
                gt = g0 + lt
                if 0 <= gt < 128:
                    xs[j, lt * 128:(lt + 1) * 128] = x_sample[0, gt * 128:(gt + 1) * 128]
                else:
                    vf[:, NT_S * (1 + j) + lt] = 0.0
            for si, lt in enumerate(EDGE_SLOTS_S):
                ja, jb = EDGE_WIN_S[lt]
                gks = [g0 + jj for jj in range(ja, jb)]
                edges[j, si, :, :, 0:len(gks), :] = _gather_na(rpb_pad, _na_index(g0 + lt, gks, SR))
        in_maps.append({
            "xp": np.ascontiguousarray(x_prompt[4 * c:4 * c + 4]),
            "xs": xs,
            "memp": np.ascontiguousarray(mem_prompt[4 * c:4 * c + 4]),
            "mems": np.ascontiguousarray(mem_sample[0]),
            "vflag": vf, "gcols": gcols, "gfin": gfin, "convp": convp, "sinkb": sinkb,
            "alibi": alibi, "alibi2": alibi2, "naint": naint, "edgep": edgep,
            "edges": np.ascontiguousarray(edges.reshape(2, 4, 128, -1)),
            "w_in_a": f(w_in_a)[0], "w_in_b": f(w_in_b)[0], "w_mem_kv": f(w_mem_kv), "w_o": f(w_o),
            "w_gate": f(w_gate), "w_up": f(w_up), "w_down": f(w_down),
        })
    res = run_bass_kernel_spmd(nc, in_maps, core_ids=list(range(NCORES)))
    y_prompt = np.concatenate([r["yp"] for r in res.results], axis=0).astype(np.float32)
    y_sample = np.concatenate([r["ys"].reshape(2048, D) for r in res.results], axis=0)[None].astype(np.float32)
    return (y_prompt, y_sample)
```
